# Optimizing a Trainium2 kernel written in Bass

```python
import math
import jax, jax.numpy as jnp
from jax import lax
import numpy as np

D_MODEL = 1024
BATCH = 4
SEQ = 8192
DEPTH = 1

HEAD_DIM = 64
MIX_WIDTH = D_MODEL
WIDTH_A = MIX_WIDTH // 2
WIDTH_B = MIX_WIDTH - WIDTH_A
N_HEADS_A = WIDTH_A // HEAD_DIM
DIFF_V_DIM = 2 * HEAD_DIM
N_HEADS_B = WIDTH_B // DIFF_V_DIM
DILATED_PATTERNS = ((128, 1), (512, 4), (2048, 16))
D_FF = 2816
CONV_WIDTH = 3
NUM_BUCKETS = 32
MAX_DISTANCE = 2048
N_BIAS_HEADS = N_HEADS_A + N_HEADS_B
Q_BLOCK = 128
EPS = 1e-6
NEG = -1e30
PROJ_WIDTH = 3 * WIDTH_A + 2 * N_HEADS_B * 2 * HEAD_DIM + WIDTH_B

kernel_name = 'hybrid_dilated_diff_attn_convffn_adaln'


def rms_norm(x, g):
    xf = x.astype(jnp.float32)
    return xf * lax.rsqrt(jnp.mean(xf * xf, axis=-1, keepdims=True) + EPS) * g


def modulate(h, shift, scale):
    return h * (1.0 + scale) + shift


def t5_bucket(rel):
    nb = NUM_BUCKETS // 2
    max_exact = nb // 2
    base = jnp.where(rel > 0, nb, 0)
    n = jnp.abs(rel)
    nf = jnp.maximum(n, 1).astype(jnp.float32)
    large = max_exact + (jnp.log(nf / max_exact) / math.log(MAX_DISTANCE / max_exact)
                         * (nb - max_exact)).astype(jnp.int32)
    large = jnp.minimum(large, nb - 1)
    return base + jnp.where(n < max_exact, n, large)


def dilated_window_attention(q, k, v, table, window, dilation):
    B, H, S, Dh = q.shape
    half = window // (2 * dilation)
    L = S // dilation
    nblk = -(-L // half)
    pad = nblk * half - L

    def by_residue(t):
        return t.reshape(B, H, L, dilation, Dh).transpose(0, 1, 3, 2, 4)

    qs = jnp.pad(by_residue(q), ((0, 0),) * 3 + ((0, pad), (0, 0))).reshape(B, H, dilation, nblk, half, Dh)

    def band(t):
        tp = jnp.pad(by_residue(t), ((0, 0),) * 3 + ((half, half + pad), (0, 0)))
        tp = tp.reshape(B, H, dilation, nblk + 2, half, Dh)
        return jnp.concatenate([tp[:, :, :, :-2], tp[:, :, :, 1:-1], tp[:, :, :, 2:]], axis=4)

    kb, vb = band(k), band(v)
    qi = jnp.arange(half)
    kt = jnp.arange(3 * half)
    rel = kt[None, :] - half - qi[:, None]
    bias = table[t5_bucket(rel * dilation)].transpose(2, 0, 1)
    kpos = jnp.arange(nblk)[:, None] * half + kt[None, :] - half
    valid = (jnp.abs(rel) <= half)[None] & ((kpos >= 0) & (kpos < L))[:, None, :]
    s = jnp.einsum('bhrnqd,bhrnkd->bhrnqk', qs, kb) * (Dh ** -0.5) + bias[None, :, None, None]
    s = jnp.where(valid, s, NEG)
    lse = jax.nn.logsumexp(s, axis=-1)
    p = jnp.exp(s - lse[..., None])
    o = jnp.einsum('bhrnqk,bhrnkd->bhrnqd', p, vb)
    o = o.reshape(B, H, dilation, nblk * half, Dh)[:, :, :, :L].transpose(0, 1, 3, 2, 4).reshape(B, H, S, Dh)
    lse = lse.reshape(B, H, dilation, nblk * half)[:, :, :, :L].transpose(0, 1, 3, 2).reshape(B, H, S)
    return o, lse


def differential_attention(q1, q2, k1, k2, v, table, lam):
    B, H, S, Dh = q1.shape
    nqb = S // Q_BLOCK
    kpos = jnp.arange(S)
    scale = Dh ** -0.5

    def block(args):
        q1b, q2b, start = args
        rel = kpos[None, :] - (start + jnp.arange(Q_BLOCK))[:, None]
        bias = table[t5_bucket(rel)].transpose(2, 0, 1)[None]
        p1 = jax.nn.softmax(jnp.einsum('bhqd,bhkd->bhqk', q1b, k1) * scale + bias, axis=-1)
        p2 = jax.nn.softmax(jnp.einsum('bhqd,bhkd->bhqk', q2b, k2) * scale + bias, axis=-1)
        return jnp.einsum('bhqk,bhkd->bhqd', p1 - lam * p2, v)

    q1s = q1.reshape(B, H, nqb, Q_BLOCK, Dh).transpose(2, 0, 1, 3, 4)
    q2s = q2.reshape(B, H, nqb, Q_BLOCK, Dh).transpose(2, 0, 1, 3, 4)
    starts = jnp.arange(nqb, dtype=jnp.int32) * Q_BLOCK
    o = lax.map(block, (q1s, q2s, starts))
    return o.transpose(1, 2, 0, 3, 4).reshape(B, H, S, v.shape[-1])


def depthwise_conv_centred(u, w, b):
    C = u.shape[-1]
    y = lax.conv_general_dilated(u, w[:, None, :].astype(u.dtype), window_strides=(1,), padding='SAME',
                                 dimension_numbers=('NWC', 'WIO', 'NWC'), feature_group_count=C)
    return y + b


def split_heads(t, n_heads, dh):
    B, S, _ = t.shape
    return t.reshape(B, S, n_heads, dh).transpose(0, 2, 1, 3)


def setup_inputs(seed: int = 0) -> dict:
    key = jax.random.key(seed)
    ks = jax.random.split(key, 24)
    nrm = jax.random.normal
    f32 = jnp.float32
    return {
        'x': nrm(ks[0], (BATCH, SEQ, D_MODEL), f32),
        'c': nrm(ks[1], (BATCH, D_MODEL), f32),
        'w_ada': nrm(ks[2], (DEPTH, D_MODEL, 6 * D_MODEL), f32) * D_MODEL ** -0.5,
        'b_ada': nrm(ks[3], (DEPTH, 6 * D_MODEL), f32) * 0.02,
        'norm1_g': 1.0 + 0.1 * nrm(ks[4], (DEPTH, D_MODEL), f32),
        'w_in': nrm(ks[5], (DEPTH, D_MODEL, PROJ_WIDTH), f32) * D_MODEL ** -0.5,
        'q_norm_a': 1.0 + 0.1 * nrm(ks[6], (DEPTH, HEAD_DIM), f32),
        'k_norm_a': 1.0 + 0.1 * nrm(ks[7], (DEPTH, HEAD_DIM), f32),
        'q_norm_b': 1.0 + 0.1 * nrm(ks[8], (DEPTH, HEAD_DIM), f32),
        'k_norm_b': 1.0 + 0.1 * nrm(ks[9], (DEPTH, HEAD_DIM), f32),
        'rel_bias': nrm(ks[10], (NUM_BUCKETS, N_BIAS_HEADS), f32) * 0.3,
        'lambda_q1': nrm(ks[11], (DEPTH, HEAD_DIM), f32) * 0.1,
        'lambda_k1': nrm(ks[12], (DEPTH, HEAD_DIM), f32) * 0.1,
        'lambda_q2': nrm(ks[13], (DEPTH, HEAD_DIM), f32) * 0.1,
        'lambda_k2': nrm(ks[14], (DEPTH, HEAD_DIM), f32) * 0.1,
        'subln_g': 1.0 + 0.1 * nrm(ks[15], (DEPTH, DIFF_V_DIM), f32),
        'w_out': nrm(ks[16], (DEPTH, MIX_WIDTH, D_MODEL), f32) * MIX_WIDTH ** -0.5,
        'norm2_g': 1.0 + 0.1 * nrm(ks[17], (DEPTH, D_MODEL), f32),
        'w_up': nrm(ks[18], (DEPTH, D_MODEL, 2 * D_FF), f32) * D_MODEL ** -0.5,
        'conv_w': nrm(ks[19], (DEPTH, CONV_WIDTH, 2 * D_FF), f32) * CONV_WIDTH ** -0.5,
        'conv_b': nrm(ks[20], (DEPTH, 2 * D_FF), f32) * 0.02,
        'w_down': nrm(ks[21], (DEPTH, D_FF, D_MODEL), f32) * D_FF ** -0.5,
    }


def reference(x, c, w_ada, b_ada, norm1_g, w_in, q_norm_a, k_norm_a, q_norm_b, k_norm_b, rel_bias,
              lambda_q1, lambda_k1, lambda_q2, lambda_k2, subln_g, w_out, norm2_g, w_up, conv_w, conv_b,
              w_down):
    B, S, D = x.shape
    out_dtype = x.dtype
    h = x.astype(jnp.float32)
    c_act = jax.nn.silu(c.astype(jnp.float32))
    table_a = rel_bias[:, :N_HEADS_A].astype(jnp.float32)
    table_b = rel_bias[:, N_HEADS_A:].astype(jnp.float32)
    qk_b = N_HEADS_B * 2 * HEAD_DIM
    splits = [WIDTH_A, 2 * WIDTH_A, 3 * WIDTH_A, 3 * WIDTH_A + qk_b, 3 * WIDTH_A + 2 * qk_b]

    for layer in range(DEPTH):
        lambda_init = 0.8 - 0.6 * math.exp(-0.3 * layer)
        mod = (c_act @ w_ada[layer] + b_ada[layer])[:, None, :]
        shift1, scale1, gate1, shift2, scale2, gate2 = jnp.split(mod, 6, axis=-1)

        hn = modulate(rms_norm(h, norm1_g[layer]), shift1, scale1)
        proj = hn @ w_in[layer]
        qa, ka, va, qb, kb, vb = jnp.split(proj, splits, axis=-1)

        qa = rms_norm(split_heads(qa, N_HEADS_A, HEAD_DIM), q_norm_a[layer])
        ka = rms_norm(split_heads(ka, N_HEADS_A, HEAD_DIM), k_norm_a[layer])
        va = split_heads(va, N_HEADS_A, HEAD_DIM)
        outs, lses = [], []
        for window, dilation in DILATED_PATTERNS:
            o, l = dilated_window_attention(qa, ka, va, table_a, window, dilation)
            outs.append(o)
            lses.append(l)
        wts = jax.nn.softmax(jnp.stack(lses, axis=0), axis=0)
        o_a = jnp.sum(jnp.stack(outs, axis=0) * wts[..., None], axis=0)
        o_a = o_a.transpose(0, 2, 1, 3).reshape(B, S, WIDTH_A)

        qb = qb.reshape(B, S, N_HEADS_B, 2, HEAD_DIM).transpose(3, 0, 2, 1, 4)
        kb = kb.reshape(B, S, N_HEADS_B, 2, HEAD_DIM).transpose(3, 0, 2, 1, 4)
        qb = rms_norm(qb, q_norm_b[layer])
        kb = rms_norm(kb, k_norm_b[layer])
        vb = split_heads(vb, N_HEADS_B, DIFF_V_DIM)
        lam = (jnp.exp(jnp.sum(lambda_q1[layer] * lambda_k1[layer]).astype(jnp.float32))
               - jnp.exp(jnp.sum(lambda_q2[layer] * lambda_k2[layer]).astype(jnp.float32)) + lambda_init)
        o_b = differential_attention(qb[0], qb[1], kb[0], kb[1], vb, table_b, lam)
        o_b = rms_norm(o_b, subln_g[layer]) * (1.0 - lambda_init)
        o_b = o_b.transpose(0, 2, 1, 3).reshape(B, S, WIDTH_B)

        mixed = jnp.concatenate([o_a, o_b], axis=-1) @ w_out[layer]
        h = h + gate1 * mixed

        hn = modulate(rms_norm(h, norm2_g[layer]), shift2, scale2)
        u = depthwise_conv_centred(hn @ w_up[layer], conv_w[layer], conv_b[layer])
        val, gate = jnp.split(u, 2, axis=-1)
        h = h + gate2 * ((jax.nn.silu(gate) * val) @ w_down[layer])

    return h.astype(out_dtype)
```

```python
import math
import os
import contextlib
import numpy as np
import concourse.bass as bass
import concourse.mybir as mybir
from concourse.bass_utils import run_bass_kernel_spmd

F32 = mybir.dt.float32
BF16 = mybir.dt.bfloat16
AF = mybir.ActivationFunctionType
ALU = mybir.AluOpType

S = 8192
OWN = 4096
NQ = 4097
DM = 1024
NFF = 22
EPS = 1e-6
NEGM = -30000.0
SW = 3104
SC0 = 1480
NCOL = 257
PATS = ((1, 32), (4, 8), (16, 2))
WIN = 456


class Buf:
    __slots__ = ("name", "last_w", "readers")

    def __init__(self, name=""):
        self.name = name
        self.last_w = None
        self.readers = []


class Op:
    __slots__ = ("eng", "fn", "deps", "idx", "needs_inc", "lane", "lane_cnt", "pos")

    def __init__(self, eng, fn, lane=None):
        self.pos = 0
        self.eng = eng
        self.fn = fn
        self.deps = []
        self.idx = 0
        self.needs_inc = False
        self.lane = lane
        self.lane_cnt = 0


class Prog:
    ENGS = ("pe", "act", "dve", "pool", "sp")

    def __init__(self, nc):
        self.nc = nc
        self.ops = {e: [] for e in self.ENGS}
        self.lanes = {}
        self.lane_last = {}
        self.pending_dmas = []
        self.bar_deps = {}
        self.rot = 0

    def op(self, eng, fn, r=(), w=(), lane=None):
        o = Op(eng, fn, lane)
        deps = []
        for b in r:
            if b.last_w is not None:
                deps.append(b.last_w)
        for b in w:
            if b.last_w is not None:
                deps.append(b.last_w)
            lastr = {}
            for d in b.readers:
                if d.eng == eng and d.lane is None and lane is None:
                    continue
                if d.lane is None and d.eng in ("pe", "act", "dve"):
                    if d.eng not in lastr or d.pos > lastr[d.eng].pos:
                        lastr[d.eng] = d
                else:
                    deps.append(d)
            deps.extend(lastr.values())
        if eng in self.bar_deps:
            deps.extend(self.bar_deps.pop(eng))
        if lane is not None and lane in self.lane_last:
            deps.append(self.lane_last[lane])
        seen = set()
        for d in deps:
            if id(d) in seen:
                continue
            seen.add(id(d))
            if d.eng == "pe" and eng == "pe" and d.lane is None and lane is None:
                continue
            o.deps.append(d)
            d.needs_inc = True
        for b in r:
            b.readers.append(o)
        for b in w:
            b.last_w = o
            b.readers = []
        if lane is not None:
            self.lanes[lane] = self.lanes.get(lane, 0) + 1
            o.lane_cnt = self.lanes[lane]
            o.needs_inc = True
            self.lane_last[lane] = o
            self.pending_dmas.append(o)
        o.pos = len(self.ops[eng])
        self.ops[eng].append(o)
        return o

    def dma(self, out, in_, r=(), w=(), lane=None, q="sp"):
        if lane is None:
            lane = "g%d" % (self.rot % 8)
            self.rot += 1
        return self.op(q, lambda e: e.dma_start(out=out, in_=in_), r=r, w=w, lane=lane)

    def barrier(self):
        lasts = [self.ops[e][-1] for e in self.ENGS if self.ops[e]]
        lasts = [o for o in lasts if o.lane is None] + self.pending_dmas
        self.pending_dmas = []
        for e in self.ENGS:
            self.bar_deps[e] = list(lasts) + self.bar_deps.get(e, [])

    def emit(self, final_wait_ops=()):
        nc = self.nc
        for o in final_wait_ops:
            o.needs_inc = True
        for e in self.ENGS:
            c = 0
            for o in self.ops[e]:
                if o.lane is None and o.needs_inc:
                    c += 1
                    o.idx = c
        lane_names = sorted(self.lanes)
        with contextlib.ExitStack() as st:
            esem = {e: st.enter_context(nc.semaphore("s_" + e)) for e in self.ENGS}
            lsem = {l: st.enter_context(nc.semaphore("l_" + l)) for l in lane_names}
            block = st.enter_context(nc.Block())

            def token(o):
                if o.lane is not None:
                    return ("L" + o.lane, lsem[o.lane], 16 * o.lane_cnt)
                return ("E" + o.eng, esem[o.eng], o.idx)

            def replay(ename, eh):
                seen = {}
                for o in self.ops[ename]:
                    for d in o.deps:
                        key, sem, val = token(d)
                        if seen.get(key, 0) >= val:
                            continue
                        seen[key] = val
                        eh.wait_ge(sem, val)
                    ins = o.fn(eh)
                    if o.needs_inc:
                        if o.lane is not None:
                            ins.then_inc(lsem[o.lane], 16)
                        else:
                            ins.then_inc(esem[ename], 1)
                if ename == "sp":
                    for o in final_wait_ops:
                        key, sem, val = token(o)
                        if seen.get(key, 0) >= val:
                            continue
                        seen[key] = val
                        eh.wait_ge(sem, val)

            @block.tensor
            def _(e):
                replay("pe", e)

            @block.scalar
            def _(e):
                replay("act", e)

            @block.vector
            def _(e):
                replay("dve", e)

            @block.gpsimd
            def _(e):
                replay("pool", e)

            @block.sync
            def _(e):
                replay("sp", e)


class Ring:
    def __init__(self, items):
        self.items = items
        self.i = 0

    def next(self):
        it = self.items[self.i % len(self.items)]
        self.i += 1
        return it


def build_nc(stop_after=None, debug=False):
    nc = bass.Bass("TRN2", target_bir_lowering=False)

    def din(name, shape, dt=F32):
        return nc.dram_tensor(name, shape, dt, kind="ExternalInput").ap()

    xT = din("xT", [8, 128, S])
    cols_d = din("cols", [128, NCOL])
    cmat_d = din("cmat", [128, 6, 128])
    w_ada_t = din("w_ada", [16, 128, 8, 384])
    w_in_t = din("w_in", [24, 128, 8, 128])
    w_out_t = din("w_out", [8, 128, 8, 128])
    w_up_t = din("w_up", [NFF // 2, 128, 8, 512])
    xw_d = din("xw", [9, 128, 8, 460])
    w_down = din("w_down", [NFF * 128, DM])
    biasA_d = din("biasA", [12, 128, 512])
    stripB_d = din("stripB", [4, 128, SW])
    cfar_d = din("cfar", [128, 8])
    outT = nc.dram_tensor("outT", [8, 128, OWN], F32, kind="ExternalOutput").ap()
    skind = dict(kind="ExternalOutput") if debug else {}
    otscr = nc.dram_tensor("otscr", [8, 128, 4104], BF16, **skind).ap()
    hscr = nc.dram_tensor("hscr", [9, 128, 8 * 460], F32, **skind).ap()
    nscr = nc.dram_tensor("nscr", [9, 128, 8 * 460], BF16, **skind).ap()
    if debug:
        dbg_mod = nc.dram_tensor("dbg_mod", [128, 64], F32, kind="ExternalOutput").ap()
        dbg_xn = nc.dram_tensor("dbg_xn", [128, 8, 512], BF16, kind="ExternalOutput").ap()

    w_down_v = w_down.rearrange("(f p) n -> p f n", p=128)
    xT_v = xT.rearrange("k p t -> p k t")
    otscr_v = otscr.rearrange("k p t -> p k t")
    hscr_w = hscr.rearrange("w p (k t) -> w p k t", k=8)
    nscr_w = nscr.rearrange("w p (k t) -> w p k t", k=8)
    outT_v = outT.rearrange("k p t -> p k t")

    P = Prog(nc)
    finals = []

    with contextlib.ExitStack() as top:
        uid = [0]

        def sbt(st, name, shape, dt):
            uid[0] += 1
            return st.enter_context(nc.sbuf_tensor("t%d_%s" % (uid[0], name), shape, dt))

        PS = top.enter_context(nc.psum_tensor("PS", [128, 8, 512], F32))
        bk = [Buf("bk%d" % i) for i in range(8)]

        cols = sbt(top, "cols", [128, NCOL], F32)
        Bcols = Buf("cols")
        cm_f = sbt(top, "cm_f", [128, 6, 128], F32)
        cm_b = sbt(top, "cm_b", [128, 6, 128], BF16)
        Bcm = Buf("cm")
        ident_b = cm_b[:, 0, :]
        ones_b = cm_b[:, 1, :]
        blk_b = cm_b[:, 2, :]
        ones_f = cm_f[:, 1, :]
        sel_f = [cm_f[0:64, 3, :], cm_f[0:64, 4, :]]
        cfar = sbt(top, "cfar", [128, 8], F32)
        modc = sbt(top, "modc", [128, 48], F32)
        Bmod = Buf("mod")
        sm = sbt(top, "sm", [128, 32], F32)
        Bsm = Buf("sm")
        G1 = sm[:, 0:8]
        G2 = sm[:, 8:16]
        gq_a = sm[:, 16:17]
        gq_b = sm[:, 17:18]
        neglam = sm[:, 18:19]
        subg = sm[:, 19:20]
        prod = sm[:, 20:22]
        e12 = sm[:, 22:24]
        c_act = sm[:, 24:32]
        gk_a = cols[:, 73:74]
        gk_b = cols[:, 75:76]
        shift1 = modc[:, 0:8]
        gate1 = modc[:, 16:24]
        shift2 = modc[:, 24:32]
        gate2 = modc[:, 40:48]

        def mm(out, lhsT, rhs, start, stop, r, w, **kw):
            return P.op("pe", lambda e: e.matmul(out, lhsT=lhsT, rhs=rhs, start=start, stop=stop, **kw), r=r, w=w)

        def act(out, in_, func, r, w, **kw):
            return P.op("act", lambda e: e.activation(out=out, in_=in_, func=func, **kw), r=r, w=w)

        def tt(eng, out, in0, in1, op, r, w):
            return P.op(eng, lambda e: e.tensor_tensor(out=out, in0=in0, in1=in1, op=op), r=r, w=w)

        def ts(eng, out, in0, s1, s2, op0, op1, r, w):
            if op1 is None:
                return P.op(eng, lambda e: e.tensor_scalar(out=out, in0=in0, scalar1=s1, scalar2=None, op0=op0), r=r, w=w)
            return P.op(eng, lambda e: e.tensor_scalar(out=out, in0=in0, scalar1=s1, scalar2=s2, op0=op0, op1=op1), r=r, w=w)

        def stt(out, in0, scalar, in1, op0, op1, r, w):
            return P.op("dve", lambda e: e.scalar_tensor_tensor(out=out, in0=in0, scalar=scalar, in1=in1, op0=op0, op1=op1), r=r, w=w)

        def cp(eng, out, in_, r, w):
            return P.op(eng, lambda e: e.tensor_copy(out=out, in_=in_), r=r, w=w)

        def rsqrt_chain(ps_ap, shape, scale, lnt, rr, r, Blnt, Brr):
            act(lnt, ps_ap, AF.Ln, r=r, w=[Blnt], scale=scale, bias=EPS)
            act(rr, lnt, AF.Exp, r=[Blnt], w=[Brr], scale=-0.5)

        P.dma(cols[:], cols_d, w=[Bcols])
        P.dma(cm_f[:], cmat_d, w=[Bcm])
        P.dma(cfar[:], cfar_d, w=[Bcols])
        cp("pool", cm_b[:], cm_f[:], r=[Bcm], w=[Bcm])
        act(c_act, cols[:, 0:8], AF.Silu, r=[Bcols], w=[Bsm])
        main = top.enter_context(contextlib.ExitStack())
        xn_lo = sbt(main, "xn_lo", [128, 8, 5632], BF16)
        xn_parts = {"lo": xn_lo, "hi": None}
        Bxn = [Buf("xn%d" % g) for g in range(16)]

        def xn_ap(g, N=512, kc=None):
            t, gg = (xn_parts["lo"], g) if g < 11 else (xn_parts["hi"], g - 11)
            if kc is None:
                return t[:, :, gg * 512:gg * 512 + N]
            return t[:, kc, gg * 512:gg * 512 + N]

        def build_xn(groups, extra=()):
            extra = list(extra)
            groups = list(groups)
            with contextlib.ExitStack() as st1:
                xs = Ring([(sbt(st1, "xs%d" % i, [128, 8, 512], F32), Buf(), "xs%d" % i) for i in range(3)])
                sq = Ring([(sbt(st1, "xsq%d" % i, [128, 8, 512], BF16), Buf()) for i in range(2)])
                ln = Ring([(sbt(st1, "xln%d" % i, [128, 512], F32), Buf()) for i in range(2)])
                rr = Ring([(sbt(st1, "xrr%d" % i, [128, 512], F32), Buf()) for i in range(2)])
                for _ in range(min(2, len(extra))):
                    extra.pop(0)()
                for gi, g in enumerate(groups):
                    xt, Bx, lane = xs.next()
                    P.dma(xt[:], xT_v[:, :, g * 512:(g + 1) * 512], w=[Bx], lane=lane)
                    sqt, Bs = sq.next()
                    act(sqt[:], xt[:], AF.Square, r=[Bx], w=[Bs])
                    b = g % 2
                    for kc in range(8):
                        mm(PS[:, b, :], ones_b, sqt[:, kc, :], kc == 0, kc == 7, r=[Bcm, Bs], w=[bk[b]])
                    lt, Bl = ln.next()
                    rt, Br = rr.next()
                    rsqrt_chain(PS[:, b, :], None, 1.0 / DM, lt[:], rt[:], [bk[b]], Bl, Br)
                    tt("dve", xn_ap(g), xt[:],
                       rt[:].unsqueeze(1).to_broadcast([128, 8, 512]), ALU.mult, r=[Bx, Br], w=[Bxn[g]])
                    left = len(groups) - gi - 1
                    k = len(extra) if left == 0 else -(-len(extra) // (left + 1))
                    for _ in range(k):
                        extra.pop(0)()
                P.barrier()

        with contextlib.ExitStack() as st0:
            wa = Ring([(sbt(st0, "wa%d" % i, [128, 8, 384], F32), Buf("wa%d" % i), "wa%d" % i) for i in range(2)])

            def ada_piece(pc):
                def f():
                    t, B, lane = wa.next()
                    P.dma(t[:], w_ada_t[pc], w=[B], lane=lane)
                    for cl in range(3):
                        cc = pc * 3 + cl
                        for kc in range(8):
                            mm(PS[:, 7, cc:cc + 1], t[:, kc, cl * 128:(cl + 1) * 128], c_act[:, kc:kc + 1],
                               kc == 0, kc == 7, r=[B, Bsm], w=[bk[7]])
                return f
            build_xn(range(0, 11), extra=[ada_piece(pc) for pc in range(16)])
            tt("dve", modc[:], PS[:, 7, 0:48], cols[:, 8:56], ALU.add, r=[bk[7], Bcols], w=[Bmod])
            stt(G1, modc[:, 8:16], 1.0, cols[:, 56:64], ALU.add, ALU.mult, r=[Bmod, Bcols], w=[Bsm])
            stt(G2, modc[:, 32:40], 1.0, cols[:, 64:72], ALU.add, ALU.mult, r=[Bmod, Bcols], w=[Bsm])
            ts("dve", gq_a, cols[:, 72:73], 0.125, None, ALU.mult, None, r=[Bcols], w=[Bsm])
            ts("dve", gq_b, cols[:, 74:75], 0.125, None, ALU.mult, None, r=[Bcols], w=[Bsm])
            ts("dve", subg, cols[:, 76:77], 0.8, None, ALU.mult, None, r=[Bcols], w=[Bsm])
            tt("dve", prod, cols[:, 77:81:2], cols[:, 78:82:2], ALU.mult, r=[Bcols], w=[Bsm])
            mm(PS[:, 6, 0:2], ones_f, prod, True, True, r=[Bcm, Bsm], w=[bk[6]])
            act(e12, PS[:, 6, 0:2], AF.Exp, r=[bk[6]], w=[Bsm])
            stt(neglam, e12[:, 1:2], 0.2, e12[:, 0:1], ALU.subtract, ALU.subtract, r=[Bsm], w=[Bsm])
            if debug:
                finals.append(P.dma(dbg_mod[:, 0:48], modc[:], r=[Bmod]))
                finals.append(P.dma(dbg_mod[:, 48:64], sm[:, 8:24], r=[Bsm]))
            P.barrier()

        def load_w3(st, col0s, name):
            wst = Ring([(sbt(st, name + "wst%d" % i, [128, 8, 128], F32), Buf(), name + "w%d" % i) for i in range(2)])
            wbf = sbt(st, name + "wbf", [128, 8, 384], BF16)
            Bw = [Buf(), Buf(), Buf()]
            b3 = sbt(st, name + "b3", [128, 4], F32)
            Bb3 = [Buf(), Buf(), Buf()]
            for ci, c0 in enumerate(col0s):
                t, B, lane = wst.next()
                P.dma(t[:], w_in_t[c0 // 128], w=[B], lane=lane)
                pbb = 5 + (ci % 2)
                for kc in range(8):
                    mm(PS[:, pbb, 100 + ci:101 + ci], t[:, kc, :], shift1[:, kc:kc + 1], kc == 0, kc == 7,
                       r=[B, Bmod], w=[bk[pbb]])
                tt("dve", wbf[:, :, ci * 128:(ci + 1) * 128], t[:],
                   G1.unsqueeze(2).to_broadcast([128, 8, 128]), ALU.mult, r=[B, Bsm], w=[Bw[ci]])
                cp("dve", b3[:, ci:ci + 1], PS[:, pbb, 100 + ci:101 + ci], r=[bk[pbb]], w=[Bb3[ci]])
            return wbf, Bw, b3, Bb3

        class QKN:
            def __init__(self, st, name):
                self.ysb = Ring([(sbt(st, name + "y%d" % i, [128, 512], F32), Buf()) for i in range(2)])
                self.sqb = Ring([(sbt(st, name + "s%d" % i, [128, 512], BF16), Buf()) for i in range(2)])
                self.ln = Ring([(sbt(st, name + "l%d" % i, [128, 512], F32), Buf()) for i in range(2)])
                self.rr = Ring([(sbt(st, name + "r%d" % i, [128, 512], F32), Buf()) for i in range(2)])
                self.nb = 0

            def run(self, pb, N, biascol, gcol, out_ap, rB, wB):
                y, By = self.ysb.next()
                act(y[:, :N], PS[:, pb, :N], AF.Identity, r=[bk[pb]] + rB, w=[By], bias=biascol)
                s, Bs = self.sqb.next()
                tt("pool", s[:, :N], y[:, :N], y[:, :N], ALU.mult, r=[By], w=[Bs])
                self.flush()

                def part2():
                    b2 = 3 + (self.nb % 2)
                    self.nb += 1
                    mm(PS[:, b2, :N], blk_b, s[:, :N], True, True, r=[Bcm, Bs], w=[bk[b2]])
                    l, Bl = self.ln.next()
                    r_, Br = self.rr.next()
                    rsqrt_chain(PS[:, b2, :N], None, 1.0 / 64, l[:, :N], r_[:, :N], [bk[b2]], Bl, Br)
                    stt(out_ap, y[:, :N], gcol, r_[:, :N], ALU.mult, ALU.mult, r=[By, Br, Bsm, Bcols], w=wB)
                self.pending = part2

            def flush(self):
                if getattr(self, "pending", None) is not None:
                    p2 = self.pending
                    self.pending = None
                    p2()

        def proj(pb, wbf, Bw, ci, g, N):
            for kc in range(8):
                mm(PS[:, pb, :N], wbf[:, kc, ci * 128:(ci + 1) * 128], xn_ap(g, N, kc),
                   kc == 0, kc == 7, r=[Bw[ci], Bxn[g]], w=[bk[pb]])

        def phase_A(hp):
            with contextlib.ExitStack() as st:
                QT = sbt(st, "aQT", [128, 4100], BF16)
                KT = sbt(st, "aKT", [128, 5632], BF16)
                VT = sbt(st, "aVT", [128, 5632], BF16)
                BQ, BK, BV = Buf(), Buf(), Buf()
                stp = st.enter_context(contextlib.ExitStack())
                wbf, Bw, b3, Bb3 = load_w3(stp, [128 * hp, 512 + 128 * hp, 1024 + 128 * hp], "a")
                qkn = QKN(stp, "a")
                pbr = 0
                for g in range(11):
                    jobs = [(1, 512), (2, 512)]
                    if g < 8:
                        jobs.insert(0, (0, 512))
                    elif g == 8:
                        jobs.append((0, 1))
                    for ci, N in jobs:
                        pb = pbr % 3
                        pbr += 1
                        proj(pb, wbf, Bw, ci, g, N)
                        if ci == 2:
                            act(VT[:, g * 512:g * 512 + N], PS[:, pb, :N], AF.Identity, r=[bk[pb], Bb3[2]], w=[BV],
                                bias=b3[:, 2:3])
                        elif ci == 1:
                            qkn.run(pb, N, b3[:, 1:2], gk_a, KT[:, g * 512:g * 512 + N], [Bb3[1]], [BK])
                        else:
                            qkn.run(pb, N, b3[:, 0:1], gq_a, QT[:, g * 512:g * 512 + N], [Bb3[0]], [BQ])
                qkn.flush()
                P.barrier()
                stp.close()
                ASTOP = os.environ.get("A_STOP", "")
                if ASTOP == "proj":
                    finals.append(P.dma(otscr_v[:, 0, 0:4100], QT[:, 0:4100], r=[BQ]))
                    finals.append(P.dma(otscr_v[:, 1, 0:4100], KT[:, 0:4100], r=[BK]))
                    finals.append(P.dma(otscr_v[:, 2, 0:4100], VT[:, 0:4100], r=[BV]))
                    P.barrier()
                    return
                accN = sbt(st, "accN", [128, 4100], F32)
                accD = sbt(st, "accD", [128, 4100], F32)
                BaN, BaD = Buf(), Buf()
                bAf = Ring([(sbt(st, "bAf%d" % i, [128, 512], F32), Buf(), "bAf%d" % i) for i in range(2)])
                bA = sbt(st, "bA", [128, 3, 512], BF16)
                BbA = Buf()
                Vt = sbt(st, "aVt", [128, 52, 128], BF16)
                BVt = Buf()
                Et = Ring([(sbt(st, "aEt%d" % i, [128, 512], BF16), Buf()) for i in range(4)])
                for p, (dil, nqb) in enumerate(PATS):
                    t, B, lane = bAf.next()
                    P.dma(t[:], biasA_d[p * 4 + hp], w=[B], lane=lane)
                    act(bA[:, p, :], t[:], AF.Exp, r=[B], w=[BbA])
                for p, (dil, nqb) in enumerate(PATS):
                    if os.environ.get("A_PATS") and str(p) not in os.environ["A_PATS"]:
                        continue
                    tiles = {}
                    lst = []
                    for r in range(dil):
                        for c in range(nqb + 1 + (1 if r == 0 else 0)):
                            tiles[(r, c)] = len(lst)
                            lst.append((r, c))
                    assert len(lst) <= 52

                    def prange(r, c):
                        if c == 0:
                            return 64, 128
                        if c == nqb + 1:
                            return 0, 32
                        return 0, 128
                    for i0 in range(0, len(lst), 8):
                        grp = lst[i0:i0 + 8]
                        tb = 6 + ((i0 // 8) % 2)
                        ptb = PS[:, tb, :].bitcast(BF16)
                        for i, (r, c) in enumerate(grp):
                            plo, phi = prange(r, c)
                            u0 = (128 * c - 64 + plo) * dil + r
                            cnt = phi - plo
                            P.op("pe", (lambda o_, i_: lambda e: e.transpose(out=o_, in_=i_, identity=ident_b))(
                                ptb[plo:phi, i * 128:(i + 1) * 128], VT[:, u0:u0 + (cnt - 1) * dil + 1:dil]),
                                r=[BV, Bcm], w=[bk[tb]])
                        cp("dve", Vt[:, i0:i0 + len(grp), :], ptb[:, 0:len(grp) * 128].rearrange("p (a b) -> p a b", b=128),
                           r=[bk[tb]], w=[BVt])
                    if os.environ.get("A_VTONLY"):
                        continue
                    units = []
                    for r in range(dil):
                        for qb in list(range(nqb)) + ([nqb] if (r == 0 and not os.environ.get("A_NOEXTRA")) else []):
                            units.append((r, qb))

                    def a_s_stage(un, r, qb):
                        N = 128 if qb < nqb else 1
                        sb_ = 2 * (un % 3)
                        uq0 = 128 * qb * dil + r
                        qsl = slice(uq0, uq0 + (N - 1) * dil + 1, dil)
                        chs = []
                        for ch in range(2):
                            c = qb + ch
                            plo, phi = prange(r, c)
                            chs.append((c, plo, phi))
                        for h in range(2):
                            for ch, (c, plo, phi) in enumerate(chs):
                                uk0 = (128 * c - 64 + plo) * dil + r
                                cnt = phi - plo
                                ksl = slice(uk0, uk0 + (cnt - 1) * dil + 1, dil)
                                mm(PS[plo:phi, sb_ + h, ch * 128:ch * 128 + N], KT[64 * h:64 * h + 64, ksl],
                                   QT[64 * h:64 * h + 64, qsl],
                                   True, True, r=[BK, BQ], w=[bk[sb_ + h]], skip_group_check=True)
                        et, Be = Et.next()
                        act(et[:].rearrange("p (a b) -> p a b", a=2), PS[:, sb_:sb_ + 2, 0:256], AF.Exp,
                            r=[bk[sb_], bk[sb_ + 1]], w=[Be])
                        tt("dve", et[:], et[:], bA[:, p, :], ALU.mult, r=[Be, BbA], w=[Be])
                        return (un, r, N, qsl, chs, et, Be)

                    def a_pv_stage(un, r, N, qsl, chs, et, Be):
                        ob_ = 6 + (un % 2)
                        for which in range(2):
                            for h in range(2):
                                for ch, (c, plo, phi) in enumerate(chs):
                                    qd = (h * 2 + ch) * 128
                                    if which == 0:
                                        lh = Vt[plo:phi, tiles[(r, c)], 64 * h:64 * h + 64]
                                    else:
                                        lh = ones_b[plo:phi, 0:64]
                                    mm(PS[64 * h:64 * h + 64, ob_, which * 128:which * 128 + N], lh,
                                       et[plo:phi, qd:qd + N], ch == 0, ch == 1,
                                       r=[BVt, Be, Bcm], w=[bk[ob_]])
                        if p == 0:
                            cp("dve", accN[:, qsl], PS[:, ob_, 0:N], r=[bk[ob_]], w=[BaN])
                            cp("dve", accD[:, qsl], PS[:, ob_, 128:128 + N], r=[bk[ob_]], w=[BaD])
                        else:
                            tt("dve", accN[:, qsl], PS[:, ob_, 0:N], accN[:, qsl], ALU.add, r=[bk[ob_], BaN], w=[BaN])
                            tt("dve", accD[:, qsl], PS[:, ob_, 128:128 + N], accD[:, qsl], ALU.add, r=[bk[ob_], BaD], w=[BaD])

                    inflight = []
                    for un, (r, qb) in enumerate(units):
                        inflight.append(a_s_stage(un, r, qb))
                        if len(inflight) > 2:
                            a_pv_stage(*inflight.pop(0))
                    while inflight:
                        a_pv_stage(*inflight.pop(0))
                ob = sbt(st, "aob", [128, 4100], BF16)
                Bob = Buf()
                for q0 in range(0, NQ, 512):
                    N = min(512, NQ - q0)
                    act(accD[:, q0:q0 + N], accD[:, q0:q0 + N], AF.Ln, r=[BaD], w=[BaD])
                    act(accD[:, q0:q0 + N], accD[:, q0:q0 + N], AF.Exp, r=[BaD], w=[BaD], scale=-1.0)
                    tt("dve", ob[:, q0:q0 + N], accN[:, q0:q0 + N], accD[:, q0:q0 + N], ALU.mult, r=[BaN, BaD], w=[Bob])
                d = P.dma(otscr_v[:, hp, 0:NQ], ob[:, 0:NQ], r=[Bob])
                if debug and stop_after == "A":
                    finals.append(d)
                P.barrier()

        def phase_B(h):
            with contextlib.ExitStack() as st:
                QT = sbt(st, "bQT", [128, 4100], BF16)
                KT = sbt(st, "bKT", [128, S], BF16)
                Vtok = sbt(st, "bVt", [128, 64, 128], BF16)
                BQ, BK, BV = Buf(), Buf(), Buf()
                with contextlib.ExitStack() as stp:
                    wbf, Bw, b3, Bb3 = load_w3(stp, [1536 + 128 * h, 2048 + 128 * h, 2560 + 128 * h], "b")
                    qkn = QKN(stp, "b")
                    vts = Ring([(sbt(stp, "bvts%d" % i, [128, 512], BF16), Buf()) for i in range(2)])
                    pbr = 0
                    for g in range(16):
                        jobs = [(1, 512), (2, 512)]
                        if g < 8:
                            jobs.insert(0, (0, 512))
                        elif g == 8:
                            jobs.append((0, 1))
                        for ci, N in jobs:
                            pb = pbr % 3
                            pbr += 1
                            proj(pb, wbf, Bw, ci, g, N)
                            if ci == 2:
                                vt, Bvt = vts.next()
                                act(vt[:], PS[:, pb, :], AF.Identity, r=[bk[pb], Bb3[2]], w=[Bvt], bias=b3[:, 2:3])
                                tb = 5 + (g % 2)
                                ptb = PS[:, tb, :].bitcast(BF16)
                                for i in range(4):
                                    P.op("pe", (lambda o_, i_: lambda e: e.transpose(out=o_, in_=i_, identity=ident_b))(
                                        ptb[:, i * 128:(i + 1) * 128], vt[:, i * 128:(i + 1) * 128]),
                                        r=[Bvt, Bcm], w=[bk[tb]])
                                cp("dve", Vtok[:, 4 * g:4 * g + 4, :], ptb[:, 0:512].rearrange("p (a b) -> p a b", b=128),
                                   r=[bk[tb]], w=[BV])
                            elif ci == 1:
                                qkn.run(pb, N, b3[:, 1:2], gk_b, KT[:, g * 512:g * 512 + N], [Bb3[1]], [BK])
                            else:
                                qkn.run(pb, N, b3[:, 0:1], gq_b, QT[:, g * 512:g * 512 + N], [Bb3[0]], [BQ])
                    qkn.flush()
                    P.barrier()
                strip = sbt(st, "strip", [128, SW], BF16)
                Bst = Buf()
                with contextlib.ExitStack() as sts:
                    stf = Ring([(sbt(sts, "stf%d" % i, [128, 776], F32), Buf(), "stf%d" % i) for i in range(2)])
                    for i in range(4):
                        t, B, lane = stf.next()
                        P.dma(t[:], stripB_d[h, :, i * 776:(i + 1) * 776], w=[B], lane=lane)
                        cp("pool", strip[:, i * 776:(i + 1) * 776], t[:], r=[B], w=[Bst])
                    P.barrier()
                Et = Ring([(sbt(st, "bEt%d" % i, [128, 2, 512], BF16), (Buf(), Buf())) for i in range(4)])
                e4 = sbt(st, "be4", [128, 2, 512], F32)
                dsb = sbt(st, "bdsb", [64, 512], F32)
                Be4, Bd = Buf(), Buf()
                accs = [(sbt(st, "bacc%d" % i, [128, 2, 512], F32), Buf()) for i in range(2)]
                osq = sbt(st, "bosq", [128, 512], BF16)
                Bosq = Buf()
                obs = Ring([(sbt(st, "bob%d" % i, [128, 512], BF16), Buf(), "bob%d" % i) for i in range(2)])
                chunks = [(q0, min(456, NQ - q0)) for q0 in range(0, NQ, 456)]
                assert sum(n for _, n in chunks) == NQ and len(chunks) == 9
                items = []
                for ci_, (q0, N) in enumerate(chunks):
                    for kb in range(64):
                        items.append((ci_, q0, N, kb))

                def s_stage(i, ci_, q0, N, kb):
                    k0 = 128 * kb
                    D = k0 - q0
                    far_pos = D - (N - 1) >= 1024
                    far_neg = D + 127 <= -1024
                    near = not (far_pos or far_neg)
                    sp = 2 * (i % 2)
                    if near:
                        c0 = SC0 - D
                        assert 0 <= c0 and c0 + N <= SW
                        for t in range(2):
                            mm(PS[:, sp + t, :N], ident_b, strip[:, c0:c0 + N], True, True, r=[Bcm, Bst], w=[bk[sp + t]])
                    for t in range(2):
                        mm(PS[:, sp + t, :N], KT[64 * t:64 * t + 64, k0:k0 + 128], QT[64 * t:64 * t + 64, q0:q0 + N],
                           not near, True, r=[BK, BQ], w=[bk[sp + t]], skip_group_check=near)
                    et, Be = Et.next()
                    if near:
                        act(et[:, :, :N], PS[:, sp:sp + 2, :N], AF.Exp, r=[bk[sp], bk[sp + 1]], w=[Be[0], Be[1]])
                    else:
                        cj = 2 * h + (1 if far_pos else 0)
                        act(et[:, :, :N], PS[:, sp:sp + 2, :N], AF.Exp, r=[bk[sp], bk[sp + 1], Bcols], w=[Be[0], Be[1]],
                            bias=cfar[:, cj:cj + 1])
                    return et, Be

                def pv_stage(ci_, q0, N, kb, et, Be):
                    for t in range(2):
                        mm(PS[:, 4 + t, :N], Vtok[:, kb, :], et[:, t, :N], kb == 0, kb == 63, r=[BV, Be[t]], w=[bk[4 + t]])
                    ac, Bac = accs[ci_ % 2]
                    if kb % 4 == 3:
                        for t in range(2):
                            mm(PS[32 * t:32 * t + 32, 6, :N], ones_b[:, 0:32], et[:, t, :N], kb == 3, kb == 63,
                               r=[Bcm, Be[t]], w=[bk[6]])
                    elif kb == 0:
                        cp("dve", ac[:, :, :N], et[:, :, :N], r=[Be[0], Be[1]], w=[Bac])
                    else:
                        tt("dve", ac[:, :, :N], et[:, :, :N], ac[:, :, :N], ALU.add, r=[Be[0], Be[1], Bac], w=[Bac])

                def fin_s0(ci_, q0, N):
                    act(e4[:, :, :N], PS[:, 4:6, :N], AF.Copy, r=[bk[4], bk[5]], w=[Be4])
                    cp("dve", dsb[:, :N], PS[0:64, 6, :N], r=[bk[6]], w=[Bd])

                def fin_d(t):
                    def f(ci_, q0, N):
                        ac, Bac = accs[ci_ % 2]
                        mm(PS[:, 7, :N], ones_f, ac[:, t, :N], True, False, r=[Bcm, Bac], w=[bk[7]])
                        mm(PS[:, 7, :N], sel_f[t], dsb[:, :N], False, True, r=[Bcm, Bd], w=[bk[7]])
                        P.op("dve", (lambda o_, i_: lambda e: e.reciprocal(out=o_, in_=i_))(ac[:, t, :N], PS[:, 7, :N]),
                             r=[bk[7]], w=[Bac])
                    return f

                def fin_s3(ci_, q0, N):
                    ac, Bac = accs[ci_ % 2]
                    tt("pool", e4[:, :, :N], e4[:, :, :N], ac[:, :, :N], ALU.mult, r=[Be4, Bac], w=[Be4])
                    stt(e4[:, 0, :N], e4[:, 1, :N], neglam, e4[:, 0, :N], ALU.mult, ALU.add, r=[Be4, Bsm], w=[Be4])
                    tt("pool", osq[:, :N], e4[:, 0, :N], e4[:, 0, :N], ALU.mult, r=[Be4], w=[Bosq])

                def fin_s4(ci_, q0, N):
                    ac, Bac = accs[ci_ % 2]
                    mm(PS[:, 7, :N], ones_b, osq[:, :N], True, True, r=[Bcm, Bosq], w=[bk[7]])
                    rsqrt_chain(PS[:, 7, :N], None, 1.0 / 128, ac[:, 0, :N], ac[:, 1, :N], [bk[7]], Bac, Bac)
                    ob, Bob, lob = obs.next()
                    stt(ob[:, :N], e4[:, 0, :N], subg, ac[:, 1, :N], ALU.mult, ALU.mult, r=[Be4, Bac, Bsm], w=[Bob])
                    d = P.dma(otscr_v[:, 4 + h, q0:q0 + N], ob[:, :N], r=[Bob], lane=lob)
                    if debug and stop_after == "B":
                        finals.append(d)

                FIN = [(0, fin_s0), (4, fin_d(0)), (10, fin_d(1)), (16, fin_s3), (22, fin_s4)]
                pend = []

                def run_pend(i):
                    while pend and pend[0][0] <= i:
                        _, fn_, fc, fq0, fN = pend.pop(0)
                        fn_(fc, fq0, fN)
                infl = []

                def retire():
                    pv = infl.pop(0)
                    pv_stage(*pv)
                    return pv
                for i, (ci_, q0, N, kb) in enumerate(items):
                    run_pend(i)
                    infl.append((ci_, q0, N, kb) + s_stage(i, ci_, q0, N, kb))
                    if len(infl) > 2:
                        pv = retire()
                        if pv[3] == 63:
                            for dl, fn_ in FIN:
                                pend.append((i + dl, fn_, pv[0], pv[1], pv[2]))
                            run_pend(i)
                prev = None
                while infl:
                    prev = retire()
                    if prev[3] == 63 and infl:
                        for dl, fn_ in FIN:
                            fn_(prev[0], prev[1], prev[2])
                run_pend(10 ** 9)
                for dl, fn_ in FIN:
                    fn_(prev[0], prev[1], prev[2])
                P.barrier()

        if debug:
            finals.append(P.dma(dbg_xn, xn_ap(0), r=[Bxn[0]]))
        for hp in range(4):
            if stop_after == "xn":
                break
            phase_A(hp)
            if debug and stop_after == "A":
                break
        if stop_after not in ("xn", "A"):
            xn_parts["hi"] = sbt(main, "xn_hi", [128, 8, 2560], BF16)
            build_xn(range(11, 16))
            for h in range(4):
                phase_B(h)
                if debug and stop_after == "B":
                    break
        main.close()
        P.barrier()

        windows = []
        lo = 0
        while lo < OWN:
            hi = min(lo + WIN, OWN)
            windows.append((lo, hi))
            lo = hi

        def stage_W():
            with contextlib.ExitStack() as st:
                wf = Ring([(sbt(st, "wof%d" % i, [128, 8, 128], F32), Buf(), "wof%d" % i) for i in range(2)])
                wob = sbt(st, "wob", [128, 8, DM], BF16)
                Bwo = [Buf() for _ in range(8)]
                for oc in range(8):
                    t, B, lane = wf.next()
                    P.dma(t[:], w_out_t[oc], w=[B], lane=lane)
                    ce = ("dve", "pool", "act")[oc % 3]
                    if ce == "act":
                        act(wob[:, :, oc * 128:(oc + 1) * 128], t[:], AF.Copy, r=[B], w=[Bwo[oc]])
                    else:
                        cp(ce, wob[:, :, oc * 128:(oc + 1) * 128], t[:], r=[B], w=[Bwo[oc]])
                otw = Ring([(sbt(st, "otw%d" % i, [128, 8, 460], BF16), Buf(), "otw%d" % i) for i in range(2)])
                xw = Ring([(sbt(st, "xw%d" % i, [128, 8, 460], F32), Buf(), "xw%d" % i) for i in range(2)])
                hT = Ring([(sbt(st, "hT%d" % i, [128, 8, 460], F32), Buf(), "hT%d" % i) for i in range(2)])
                sq = Ring([(sbt(st, "wsq%d" % i, [128, 8, 460], BF16), Buf()) for i in range(2)])
                pendW = [None]
                ln = Ring([(sbt(st, "wln%d" % i, [128, 460], F32), Buf()) for i in range(2)])
                rr = Ring([(sbt(st, "wrr%d" % i, [128, 460], F32), Buf()) for i in range(2)])
                hn = Ring([(sbt(st, "whn%d" % i, [128, 8, 460], BF16), Buf(), "whn%d" % i) for i in range(2)])
                pbr = 0

                def loadW(wi):
                    lo, hi = windows[wi]
                    t0 = max(lo - 1, 0)
                    t1 = hi + 1
                    N = t1 - t0
                    ot, Bot, lot = otw.next()
                    P.dma(ot[:, :, :N], otscr_v[:, :, t0:t1], w=[Bot], lane=lot)
                    x_, Bx, lx = xw.next()
                    P.dma(x_[:], xw_d[wi], w=[Bx], lane=lx)
                    return ot, Bot, x_, Bx
                nxtW = loadW(0)
                for wi, (lo, hi) in enumerate(windows):
                    t0 = max(lo - 1, 0)
                    t1 = hi + 1
                    N = t1 - t0
                    ot, Bot, x_, Bx = nxtW
                    if wi + 1 < len(windows):
                        nxtW = loadW(wi + 1)
                    h_, Bh, lh = hT.next()
                    for oc in range(8):
                        pb = pbr % 3
                        pbr += 1
                        for kc in range(8):
                            mm(PS[:, pb, :N], wob[:, kc, oc * 128:(oc + 1) * 128], ot[:, kc, :N], kc == 0, kc == 7,
                               r=[Bwo[oc], Bot], w=[bk[pb]])
                        stt(h_[:, oc, :N], PS[:, pb, :N], gate1[:, oc:oc + 1], x_[:, oc, :N], ALU.mult, ALU.add,
                            r=[bk[pb], Bmod, Bx], w=[Bh])
                    s_, Bs = sq.next()
                    act(s_[:, :, :N], h_[:, :, :N], AF.Square, r=[Bh], w=[Bs])
                    if pendW[0] is not None:
                        pendW[0]()

                    def partB(wi=wi, lo=lo, hi=hi, t0=t0, N=N, s_=s_, Bs=Bs, h_=h_, Bh=Bh, lh=lh):
                        nb = 3 + (wi % 2)
                        for oc in range(8):
                            mm(PS[:, nb, :N], ones_b, s_[:, oc, :N], oc == 0, oc == 7, r=[Bcm, Bs], w=[bk[nb]])
                        l_, Bl = ln.next()
                        r_, Br = rr.next()
                        rsqrt_chain(PS[:, nb, :N], None, 1.0 / DM, l_[:, :N], r_[:, :N], [bk[nb]], Bl, Br)
                        n_, Bn, lnn = hn.next()
                        tt("dve", n_[:, :, :N], h_[:, :, :N], r_[:, :N].unsqueeze(1).to_broadcast([128, 8, N]), ALU.mult,
                           r=[Bh, Br], w=[Bn])
                        P.dma(hscr_w[wi], h_[:], r=[Bh], lane=lh + "s")
                        P.dma(nscr_w[wi], n_[:], r=[Bn], lane=lnn + "s")
                    pendW[0] = partB
                pendW[0]()
                P.barrier()

        def stage_F():
            with contextlib.ExitStack() as st:
                wub = sbt(st, "wub", [128, 8, 2 * NFF * 128], BF16)
                wdb = sbt(st, "wdb", [128, NFF, DM], BF16)
                bup = sbt(st, "bup", [128, 2 * NFF], F32)
                Bwu, Bwd, Bbu = Buf(), Buf(), Buf()
                with contextlib.ExitStack() as stw:
                    wf = Ring([(sbt(stw, "wuf%d" % i, [128, 8, 512], F32), Buf(), "wuf%d" % i) for i in range(2)])
                    wdf = Ring([(sbt(stw, "wdf%d" % i, [128, DM], F32), Buf(), "wdf%d" % i) for i in range(2)])
                    for pc in range(NFF // 2):
                        t, B, lane = wf.next()
                        P.dma(t[:], w_up_t[pc], w=[B], lane=lane)
                        for cl in range(4):
                            cc = pc * 4 + cl
                            for kc in range(8):
                                mm(PS[:, 7, cc:cc + 1], t[:, kc, cl * 128:(cl + 1) * 128], shift2[:, kc:kc + 1],
                                   kc == 0, kc == 7, r=[B, Bmod], w=[bk[7]])
                        tt("dve", wub[:, :, pc * 512:(pc + 1) * 512], t[:],
                           G2.unsqueeze(2).to_broadcast([128, 8, 512]), ALU.mult, r=[B, Bsm], w=[Bwu])
                        for pd in (2 * pc, 2 * pc + 1):
                            t, B, lane = wdf.next()
                            P.dma(t[:], w_down_v[:, pd, :], w=[B], lane=lane)
                            cp("pool", wdb[:, pd, :], t[:], r=[B], w=[Bwd])
                    cp("dve", bup[:], PS[:, 7, 0:2 * NFF], r=[bk[7]], w=[Bbu])
                    P.barrier()
                K1 = sbt(st, "fK1", [128, 2 * NFF], F32)
                K0 = sbt(st, "fK0", [128, 2 * NFF], F32)
                BK1 = Buf()
                w3 = cols[:, 81:213].rearrange("p (c t) -> p c t", t=3)
                tt("dve", K1[:], w3[:, :, 0], w3[:, :, 1], ALU.add, r=[Bcols], w=[BK1])
                tt("dve", K1[:], K1[:], w3[:, :, 2], ALU.add, r=[Bcols, BK1], w=[BK1])
                tt("dve", K1[:], K1[:], bup[:], ALU.mult, r=[Bbu, BK1], w=[BK1])
                tt("dve", K1[:], K1[:], cols[:, 213:257], ALU.add, r=[Bcols, BK1], w=[BK1])
                tt("dve", K0[:], bup[:], w3[:, :, 0], ALU.mult, r=[Bbu, Bcols], w=[BK1])
                hw = Ring([(sbt(st, "fhw%d" % i, [128, 8, 460], BF16), Buf(), "fhw%d" % i) for i in range(2)])
                aT = sbt(st, "faT", [128, NFF, 460], BF16)
                BaT = Buf()
                cv = Ring([(sbt(st, "fcv%d" % i, [128, 460], F32), Buf()) for i in range(2)])
                cg = Ring([(sbt(st, "fcg%d" % i, [128, 460], F32), Buf()) for i in range(2)])
                sg = Ring([(sbt(st, "fsg%d" % i, [128, 460], F32), Buf()) for i in range(2)])
                hr = Ring([(sbt(st, "fhr%d" % i, [128, 460], F32), Buf(), "fhr%d" % i) for i in range(8)])
                ot = Ring([(sbt(st, "fot%d" % i, [128, 460], F32), Buf(), "fot%d" % i) for i in range(3)])
                pbr = 0

                def wgeom(wi):
                    lo, hi = windows[wi]
                    t0 = max(lo - 1, 0)
                    t1 = hi + 1
                    return lo, hi, t0, t1, t1 - t0, hi - lo

                def load_hn(wi):
                    lo, hi, t0, t1, N, M = wgeom(wi)
                    hn, Bhn, lhn = hw.next()
                    P.dma(hn[:], nscr_w[wi], w=[Bhn], lane=lhn)
                    return hn, Bhn
                nxt = load_hn(0)
                for wi in range(len(windows)):
                    lo, hi, t0, t1, N, M = wgeom(wi)
                    first = (wi == 0)
                    hn, Bhn = nxt
                    if wi + 1 < len(windows):
                        nxt = load_hn(wi + 1)
                    hres = []
                    for oc in range(8):
                        h_, Bh, lh = hr.next()
                        P.dma(h_[:, :M], hscr_w[wi][:, oc, lo - t0:hi - t0], w=[Bh], lane=lh)
                        hres.append((h_, Bh))
                    for f in range(NFF):
                        res = []
                        for which in range(2):
                            pb = pbr % 6
                            pbr += 1
                            cc = which * NFF + f
                            for kc in range(8):
                                mm(PS[:, pb, :N], wub[:, kc, cc * 128:(cc + 1) * 128], hn[:, kc, :N], kc == 0, kc == 7,
                                   r=[Bwu, Bhn], w=[bk[pb]])
                            c_, Bc = (cv if which == 0 else cg).next()
                            wcol = 81 + cc * 3
                            ctr = 0 if first else 1
                            act(c_[:, :M], PS[:, pb, ctr:ctr + M], AF.Identity, r=[bk[pb], BK1, Bcols], w=[Bc],
                                scale=cols[:, wcol + 1:wcol + 2], bias=K1[:, cc:cc + 1])
                            if first:
                                stt(c_[:, 1:M], PS[:, pb, 0:M - 1], cols[:, wcol:wcol + 1], c_[:, 1:M], ALU.mult, ALU.add,
                                    r=[bk[pb], Bcols, Bc], w=[Bc])
                                ts("dve", c_[:, 0:1], c_[:, 0:1], K0[:, cc:cc + 1], None, ALU.subtract, None, r=[Bc, BK1], w=[Bc])
                            else:
                                stt(c_[:, :M], PS[:, pb, 0:M], cols[:, wcol:wcol + 1], c_[:, :M], ALU.mult, ALU.add,
                                    r=[bk[pb], Bcols, Bc], w=[Bc])
                            stt(c_[:, :M], PS[:, pb, ctr + 1:ctr + 1 + M], cols[:, wcol + 2:wcol + 3], c_[:, :M], ALU.mult, ALU.add,
                                r=[bk[pb], Bcols, Bc], w=[Bc])
                            res.append((c_, Bc))
                        (yv, Byv), (yg, Byg) = res
                        s_, Bs = sg.next()
                        act(s_[:, :M], yg[:, :M], AF.Silu, r=[Byg], w=[Bs])
                        tt("pool", aT[:, f, :M], s_[:, :M], yv[:, :M], ALU.mult, r=[Bs, Byv], w=[BaT])
                    for oc in range(8):
                        pb = 6 + (oc % 2)
                        for f in range(NFF):
                            mm(PS[:, pb, :M], wdb[:, f, oc * 128:(oc + 1) * 128], aT[:, f, :M], f == 0, f == NFF - 1,
                               r=[Bwd, BaT], w=[bk[pb]])
                        h_, Bh = hres[oc]
                        o_, Bo, lo_ = ot.next()
                        stt(o_[:, :M], PS[:, pb, :M], gate2[:, oc:oc + 1], h_[:, :M], ALU.mult, ALU.add,
                            r=[bk[pb], Bmod, Bh], w=[Bo])
                        finals.append(P.dma(outT_v[:, oc, lo:hi], o_[:, :M], r=[Bo], lane=lo_))
                P.barrier()

        if stop_after is None or stop_after in ("W", "F"):
            stage_W()
        if stop_after is None or stop_after == "F":
            stage_F()
        last = {}
        for o in finals:
            last[o.lane] = o if (o.lane not in last or o.lane_cnt > last[o.lane].lane_cnt) else last[o.lane]
        with nc.allow_non_contiguous_dma(reason="single-token halo columns"):
            P.emit(final_wait_ops=list(last.values()))
    return nc


def _t5_bucket(rel):
    nb = 16
    me = 8
    base = np.where(rel > 0, nb, 0)
    n = np.abs(rel)
    nf = np.maximum(n, 1).astype(np.float32)
    large = me + (np.log(nf / np.float32(me)) / np.float32(math.log(2048 / me)) * np.float32(nb - me)).astype(np.int32)
    large = np.minimum(large, nb - 1)
    return base + np.where(n < me, n, large)


def _col(v, n):
    return np.ascontiguousarray(np.asarray(v, np.float32).reshape(n, 128).T)


def _core_inputs(inp, b, flip):
    sg = -1 if flip else 1
    x = inp["x"][b]
    if flip:
        x = x[::-1]
    xT = np.ascontiguousarray(x.T).reshape(8, 128, S)
    cols = np.zeros((128, NCOL), np.float32)
    cols[:, 0:8] = _col(inp["c"][b], 8)
    cols[:, 8:56] = _col(inp["b_ada"][0], 48)
    cols[:, 56:64] = _col(inp["norm1_g"][0], 8)
    cols[:, 64:72] = _col(inp["norm2_g"][0], 8)
    cols[:, 72] = np.tile(inp["q_norm_a"][0], 2)
    cols[:, 73] = np.tile(inp["k_norm_a"][0], 2)
    cols[:, 74] = np.tile(inp["q_norm_b"][0], 2)
    cols[:, 75] = np.tile(inp["k_norm_b"][0], 2)
    cols[:, 76] = inp["subln_g"][0]
    cols[0:64, 77] = inp["lambda_q1"][0]
    cols[0:64, 78] = inp["lambda_k1"][0]
    cols[0:64, 79] = inp["lambda_q2"][0]
    cols[0:64, 80] = inp["lambda_k2"][0]
    cw = inp["conv_w"][0]
    if flip:
        cw = cw[::-1]
    cols[:, 81:213] = np.stack([_col(cw[t], 44) for t in range(3)], axis=2).reshape(128, 132)
    cols[:, 213:257] = _col(inp["conv_b"][0], 44)
    cmat = np.zeros((128, 6, 128), np.float32)
    cmat[0:32, 3, :] = 1.0 / 32
    cmat[32:64, 4, :] = 1.0 / 32
    cmat[:, 0, :] = np.eye(128)
    cmat[:, 1, :] = 1.0
    blk = np.zeros((128, 128), np.float32)
    blk[:64, :64] = 1.0
    blk[64:, 64:] = 1.0
    cmat[:, 2, :] = blk
    table = np.asarray(inp["rel_bias"], np.float32)
    biasA = np.empty((12, 128, 512), np.float32)
    i = np.arange(128)[:, None]
    j = np.arange(128)[None, :]
    for p, (dil, _) in enumerate(PATS):
        for hp in range(4):
            for hl in range(2):
                for ch in range(2):
                    rel = (i - 64 - j) if ch == 0 else (i + 64 - j)
                    valid = (i >= j) if ch == 0 else (i <= j)
                    vals = table[_t5_bucket(sg * rel * dil), 2 * hp + hl]
                    q = hl * 2 + ch
                    biasA[p * 4 + hp][:, q * 128:(q + 1) * 128] = np.where(valid, vals, np.float32(NEGM))
    kl = np.arange(128)[:, None]
    c = np.arange(SW)[None, :]
    bidx = _t5_bucket(sg * (kl - c + SC0))
    stripB = np.stack([table[bidx, 8 + h] for h in range(4)], 0).astype(np.float32)
    cfar = np.empty((128, 8), np.float32)
    for h in range(4):
        cfar[:, 2 * h] = table[_t5_bucket(np.array(sg * -5000))[()], 8 + h]
        cfar[:, 2 * h + 1] = table[_t5_bucket(np.array(sg * 5000))[()], 8 + h]
    def tiled(w, width):
        n = w.shape[1]
        return np.ascontiguousarray(np.asarray(w, np.float32).reshape(8, 128, n // width, width).transpose(2, 1, 0, 3))
    xw = np.zeros((9, 128, 8, 460), np.float32)
    lo = 0
    wi = 0
    while lo < OWN:
        hi = min(lo + WIN, OWN)
        t0 = max(lo - 1, 0)
        t1 = hi + 1
        xw[wi, :, :, :t1 - t0] = xT[:, :, t0:t1].transpose(1, 0, 2)
        lo = hi
        wi += 1
    return {
        "xT": xT, "cols": cols, "cmat": cmat, "xw": xw,
        "w_ada": tiled(inp["w_ada"][0], 384), "w_in": tiled(inp["w_in"][0], 128),
        "w_out": tiled(inp["w_out"][0], 128), "w_up": tiled(inp["w_up"][0], 512),
        "w_down": np.ascontiguousarray(inp["w_down"][0]),
        "biasA": biasA, "stripB": stripB, "cfar": cfar,
    }


_NC_CACHE = {}


def kernel(**inputs):
    inp = {k: np.asarray(v) for k, v in inputs.items()}
    if "nc" not in _NC_CACHE:
        _NC_CACHE["nc"] = build_nc()
    nc = _NC_CACHE["nc"]
    in_maps = []
    for core in range(8):
        b, jj = core // 2, core % 2
        in_maps.append(_core_inputs(inp, b, jj == 1))
    res = run_bass_kernel_spmd(nc, in_maps, core_ids=list(range(8)))
    out = np.empty((4, S, DM), np.float32)
    for core in range(8):
        b, jj = core // 2, core % 2
        oT = np.asarray(res.results[core]["outT"]).reshape(DM, OWN)
        o = oT.T
        if jj == 1:
            out[b, OWN:] = o[::-1]
        else:
            out[b, :OWN] = o
    return out
```

```python
import math
import os
import contextlib
import numpy as np
import concourse.bass as bass
import concourse.mybir as mybir
from concourse.bass_utils import run_bass_kernel_spmd

F32 = mybir.dt.float32
BF16 = mybir.dt.bfloat16
AF = mybir.ActivationFunctionType
ALU = mybir.AluOpType

S = 8192
OWN = 4096
NQ = 4097
DM = 1024
NFF = 22
EPS = 1e-6
NEGM = -30000.0
SW = 3104
SC0 = 1480
NCOL = 257
PATS = ((1, 32), (4, 8), (16, 2))
WIN = 456


class Buf:
    __slots__ = ("name", "last_w", "readers")

    def __init__(self, name=""):
        self.name = name
        self.last_w = None
        self.readers = []


class Op:
    __slots__ = ("eng", "fn", "deps", "idx", "needs_inc", "lane", "lane_cnt", "pos")

    def __init__(self, eng, fn, lane=None):
        self.pos = 0
        self.eng = eng
        self.fn = fn
        self.deps = []
        self.idx = 0
        self.needs_inc = False
        self.lane = lane
        self.lane_cnt = 0


class Prog:
    ENGS = ("pe", "act", "dve", "pool", "sp")

    def __init__(self, nc):
        self.nc = nc
        self.ops = {e: [] for e in self.ENGS}
        self.lanes = {}
        self.lane_last = {}
        self.pending_dmas = []
        self.bar_deps = {}
        self.rot = 0

    def op(self, eng, fn, r=(), w=(), lane=None):
        o = Op(eng, fn, lane)
        deps = []
        for b in r:
            if b.last_w is not None:
                deps.append(b.last_w)
        for b in w:
            if b.last_w is not None:
                deps.append(b.last_w)
            lastr = {}
            for d in b.readers:
                if d.eng == eng and d.lane is None and lane is None:
                    continue
                if d.lane is None and d.eng in ("pe", "act", "dve"):
                    if d.eng not in lastr or d.pos > lastr[d.eng].pos:
                        lastr[d.eng] = d
                else:
                    deps.append(d)
            deps.extend(lastr.values())
        if eng in self.bar_deps:
            deps.extend(self.bar_deps.pop(eng))
        if lane is not None and lane in self.lane_last:
            deps.append(self.lane_last[lane])
        seen = set()
        for d in deps:
            if id(d) in seen:
                continue
            seen.add(id(d))
            if d.eng == "pe" and eng == "pe" and d.lane is None and lane is None:
                continue
            o.deps.append(d)
            d.needs_inc = True
        for b in r:
            b.readers.append(o)
        for b in w:
            b.last_w = o
            b.readers = []
        if lane is not None:
            self.lanes[lane] = self.lanes.get(lane, 0) + 1
            o.lane_cnt = self.lanes[lane]
            o.needs_inc = True
            self.lane_last[lane] = o
            self.pending_dmas.append(o)
        o.pos = len(self.ops[eng])
        self.ops[eng].append(o)
        return o

    def dma(self, out, in_, r=(), w=(), lane=None, q="sp"):
        if lane is None:
            lane = "g%d" % (self.rot % 8)
            self.rot += 1
        return self.op(q, lambda e: e.dma_start(out=out, in_=in_), r=r, w=w, lane=lane)

    def barrier(self):
        lasts = [self.ops[e][-1] for e in self.ENGS if self.ops[e]]
        lasts = [o for o in lasts if o.lane is None] + self.pending_dmas
        self.pending_dmas = []
        for e in self.ENGS:
            self.bar_deps[e] = list(lasts) + self.bar_deps.get(e, [])

    def emit(self, final_wait_ops=()):
        nc = self.nc
        for o in final_wait_ops:
            o.needs_inc = True
        for e in self.ENGS:
            c = 0
            for o in self.ops[e]:
                if o.lane is None and o.needs_inc:
                    c += 1
                    o.idx = c
        lane_names = sorted(self.lanes)
        with contextlib.ExitStack() as st:
            esem = {e: st.enter_context(nc.semaphore("s_" + e)) for e in self.ENGS}
            lsem = {l: st.enter_context(nc.semaphore("l_" + l)) for l in lane_names}
            block = st.enter_context(nc.Block())

            def token(o):
                if o.lane is not None:
                    return ("L" + o.lane, lsem[o.lane], 16 * o.lane_cnt)
                return ("E" + o.eng, esem[o.eng], o.idx)

            def replay(ename, eh):
                seen = {}
                for o in self.ops[ename]:
                    for d in o.deps:
                        key, sem, val = token(d)
                        if seen.get(key, 0) >= val:
                            continue
                        seen[key] = val
                        eh.wait_ge(sem, val)
                    ins = o.fn(eh)
                    if o.needs_inc:
                        if o.lane is not None:
                            ins.then_inc(lsem[o.lane], 16)
                        else:
                            ins.then_inc(esem[ename], 1)
                if ename == "sp":
                    for o in final_wait_ops:
                        key, sem, val = token(o)
                        if seen.get(key, 0) >= val:
                            continue
                        seen[key] = val
                        eh.wait_ge(sem, val)

            @block.tensor
            def _(e):
                replay("pe", e)

            @block.scalar
            def _(e):
                replay("act", e)

            @block.vector
            def _(e):
                replay("dve", e)

            @block.gpsimd
            def _(e):
                replay("pool", e)

            @block.sync
            def _(e):
                replay("sp", e)


class Ring:
    def __init__(self, items):
        self.items = items
        self.i = 0

    def next(self):
        it = self.items[self.i % len(self.items)]
        self.i += 1
        return it


def build_nc(stop_after=None, debug=False):
    nc = bass.Bass("TRN2", target_bir_lowering=False)

    def din(name, shape, dt=F32):
        return nc.dram_tensor(name, shape, dt, kind="ExternalInput").ap()

    xT = din("xT", [8, 128, S])
    cols_d = din("cols", [128, NCOL])
    cmat_d = din("cmat", [128, 6, 128])
    w_ada_t = din("w_ada", [16, 128, 8, 384])
    w_in_t = din("w_in", [24, 128, 8, 128])
    w_out_t = din("w_out", [8, 128, 8, 128])
    w_up_t = din("w_up", [NFF // 2, 128, 8, 512])
    xw_d = din("xw", [9, 128, 8, 460])
    w_down = din("w_down", [NFF * 128, DM])
    biasA_d = din("biasA", [12, 128, 512])
    stripB_d = din("stripB", [4, 128, SW])
    cfar_d = din("cfar", [128, 8])
    outT = nc.dram_tensor("outT", [8, 128, OWN], F32, kind="ExternalOutput").ap()
    skind = dict(kind="ExternalOutput") if debug else {}
    otscr = nc.dram_tensor("otscr", [8, 128, 4104], BF16, **skind).ap()
    hscr = nc.dram_tensor("hscr", [9, 128, 8 * 460], F32, **skind).ap()
    nscr = nc.dram_tensor("nscr", [9, 128, 8 * 460], BF16, **skind).ap()
    if debug:
        dbg_mod = nc.dram_tensor("dbg_mod", [128, 64], F32, kind="ExternalOutput").ap()
        dbg_xn = nc.dram_tensor("dbg_xn", [128, 8, 512], BF16, kind="ExternalOutput").ap()

    w_down_v = w_down.rearrange("(f p) n -> p f n", p=128)
    xT_v = xT.rearrange("k p t -> p k t")
    otscr_v = otscr.rearrange("k p t -> p k t")
    hscr_w = hscr.rearrange("w p (k t) -> w p k t", k=8)
    nscr_w = nscr.rearrange("w p (k t) -> w p k t", k=8)
    outT_v = outT.rearrange("k p t -> p k t")

    P = Prog(nc)
    finals = []

    with contextlib.ExitStack() as top:
        uid = [0]

        def sbt(st, name, shape, dt):
            uid[0] += 1
            return st.enter_context(nc.sbuf_tensor("t%d_%s" % (uid[0], name), shape, dt))

        PS = top.enter_context(nc.psum_tensor("PS", [128, 8, 512], F32))
        bk = [Buf("bk%d" % i) for i in range(8)]

        cols = sbt(top, "cols", [128, NCOL], F32)
        Bcols = Buf("cols")
        cm_f = sbt(top, "cm_f", [128, 6, 128], F32)
        cm_b = sbt(top, "cm_b", [128, 6, 128], BF16)
        Bcm = Buf("cm")
        ident_b = cm_b[:, 0, :]
        ones_b = cm_b[:, 1, :]
        blk_b = cm_b[:, 2, :]
        ones_f = cm_f[:, 1, :]
        sel_f = [cm_f[0:64, 3, :], cm_f[0:64, 4, :]]
        cfar = sbt(top, "cfar", [128, 8], F32)
        modc = sbt(top, "modc", [128, 48], F32)
        Bmod = Buf("mod")
        sm = sbt(top, "sm", [128, 32], F32)
        Bsm = Buf("sm")
        G1 = sm[:, 0:8]
        G2 = sm[:, 8:16]
        gq_a = sm[:, 16:17]
        gq_b = sm[:, 17:18]
        neglam = sm[:, 18:19]
        subg = sm[:, 19:20]
        prod = sm[:, 20:22]
        e12 = sm[:, 22:24]
        c_act = sm[:, 24:32]
        gk_a = cols[:, 73:74]
        gk_b = cols[:, 75:76]
        shift1 = modc[:, 0:8]
        gate1 = modc[:, 16:24]
        shift2 = modc[:, 24:32]
        gate2 = modc[:, 40:48]

        def mm(out, lhsT, rhs, start, stop, r, w, **kw):
            return P.op("pe", lambda e: e.matmul(out, lhsT=lhsT, rhs=rhs, start=start, stop=stop, **kw), r=r, w=w)

        def act(out, in_, func, r, w, **kw):
            return P.op("act", lambda e: e.activation(out=out, in_=in_, func=func, **kw), r=r, w=w)

        def tt(eng, out, in0, in1, op, r, w):
            return P.op(eng, lambda e: e.tensor_tensor(out=out, in0=in0, in1=in1, op=op), r=r, w=w)

        def ts(eng, out, in0, s1, s2, op0, op1, r, w):
            if op1 is None:
                return P.op(eng, lambda e: e.tensor_scalar(out=out, in0=in0, scalar1=s1, scalar2=None, op0=op0), r=r, w=w)
            return P.op(eng, lambda e: e.tensor_scalar(out=out, in0=in0, scalar1=s1, scalar2=s2, op0=op0, op1=op1), r=r, w=w)

        def stt(out, in0, scalar, in1, op0, op1, r, w):
            return P.op("dve", lambda e: e.scalar_tensor_tensor(out=out, in0=in0, scalar=scalar, in1=in1, op0=op0, op1=op1), r=r, w=w)

        def cp(eng, out, in_, r, w):
            return P.op(eng, lambda e: e.tensor_copy(out=out, in_=in_), r=r, w=w)

        def rsqrt_chain(ps_ap, shape, scale, lnt, rr, r, Blnt, Brr):
            act(lnt, ps_ap, AF.Ln, r=r, w=[Blnt], scale=scale, bias=EPS)
            act(rr, lnt, AF.Exp, r=[Blnt], w=[Brr], scale=-0.5)

        P.dma(cols[:], cols_d, w=[Bcols])
        P.dma(cm_f[:], cmat_d, w=[Bcm])
        P.dma(cfar[:], cfar_d, w=[Bcols])
        cp("pool", cm_b[:], cm_f[:], r=[Bcm], w=[Bcm])
        act(c_act, cols[:, 0:8], AF.Silu, r=[Bcols], w=[Bsm])
        main = top.enter_context(contextlib.ExitStack())
        xn_lo = sbt(main, "xn_lo", [128, 8, 5632], BF16)
        xn_parts = {"lo": xn_lo, "hi": None}
        Bxn = [Buf("xn%d" % g) for g in range(16)]

        def xn_ap(g, N=512, kc=None):
            t, gg = (xn_parts["lo"], g) if g < 11 else (xn_parts["hi"], g - 11)
            if kc is None:
                return t[:, :, gg * 512:gg * 512 + N]
            return t[:, kc, gg * 512:gg * 512 + N]

        def build_xn(groups, extra=()):
            extra = list(extra)
            groups = list(groups)
            with contextlib.ExitStack() as st1:
                xs = Ring([(sbt(st1, "xs%d" % i, [128, 8, 512], F32), Buf(), "xs%d" % i) for i in range(3)])
                sq = Ring([(sbt(st1, "xsq%d" % i, [128, 8, 512], BF16), Buf()) for i in range(2)])
                ln = Ring([(sbt(st1, "xln%d" % i, [128, 512], F32), Buf()) for i in range(2)])
                rr = Ring([(sbt(st1, "xrr%d" % i, [128, 512], F32), Buf()) for i in range(2)])
                for _ in range(min(2, len(extra))):
                    extra.pop(0)()
                for gi, g in enumerate(groups):
                    xt, Bx, lane = xs.next()
                    P.dma(xt[:], xT_v[:, :, g * 512:(g + 1) * 512], w=[Bx], lane=lane)
                    sqt, Bs = sq.next()
                    act(sqt[:], xt[:], AF.Square, r=[Bx], w=[Bs])
                    b = g % 2
                    for kc in range(8):
                        mm(PS[:, b, :], ones_b, sqt[:, kc, :], kc == 0, kc == 7, r=[Bcm, Bs], w=[bk[b]])
                    lt, Bl = ln.next()
                    rt, Br = rr.next()
                    rsqrt_chain(PS[:, b, :], None, 1.0 / DM, lt[:], rt[:], [bk[b]], Bl, Br)
                    tt("dve", xn_ap(g), xt[:],
                       rt[:].unsqueeze(1).to_broadcast([128, 8, 512]), ALU.mult, r=[Bx, Br], w=[Bxn[g]])
                    left = len(groups) - gi - 1
                    k = len(extra) if left == 0 else -(-len(extra) // (left + 1))
                    for _ in range(k):
                        extra.pop(0)()
                P.barrier()

        with contextlib.ExitStack() as st0:
            wa = Ring([(sbt(st0, "wa%d" % i, [128, 8, 384], F32), Buf("wa%d" % i), "wa%d" % i) for i in range(2)])

            def ada_piece(pc):
                def f():
                    t, B, lane = wa.next()
                    P.dma(t[:], w_ada_t[pc], w=[B], lane=lane)
                    for cl in range(3):
                        cc = pc * 3 + cl
                        for kc in range(8):
                            mm(PS[:, 7, cc:cc + 1], t[:, kc, cl * 128:(cl + 1) * 128], c_act[:, kc:kc + 1],
                               kc == 0, kc == 7, r=[B, Bsm], w=[bk[7]])
                return f
            build_xn(range(0, 11), extra=[ada_piece(pc) for pc in range(16)])
            tt("dve", modc[:], PS[:, 7, 0:48], cols[:, 8:56], ALU.add, r=[bk[7], Bcols], w=[Bmod])
            stt(G1, modc[:, 8:16], 1.0, cols[:, 56:64], ALU.add, ALU.mult, r=[Bmod, Bcols], w=[Bsm])
            stt(G2, modc[:, 32:40], 1.0, cols[:, 64:72], ALU.add, ALU.mult, r=[Bmod, Bcols], w=[Bsm])
            ts("dve", gq_a, cols[:, 72:73], 0.125, None, ALU.mult, None, r=[Bcols], w=[Bsm])
            ts("dve", gq_b, cols[:, 74:75], 0.125, None, ALU.mult, None, r=[Bcols], w=[Bsm])
            ts("dve", subg, cols[:, 76:77], 0.8, None, ALU.mult, None, r=[Bcols], w=[Bsm])
            tt("dve", prod, cols[:, 77:81:2], cols[:, 78:82:2], ALU.mult, r=[Bcols], w=[Bsm])
            mm(PS[:, 6, 0:2], ones_f, prod, True, True, r=[Bcm, Bsm], w=[bk[6]])
            act(e12, PS[:, 6, 0:2], AF.Exp, r=[bk[6]], w=[Bsm])
            stt(neglam, e12[:, 1:2], 0.2, e12[:, 0:1], ALU.subtract, ALU.subtract, r=[Bsm], w=[Bsm])
            if debug:
                finals.append(P.dma(dbg_mod[:, 0:48], modc[:], r=[Bmod]))
                finals.append(P.dma(dbg_mod[:, 48:64], sm[:, 8:24], r=[Bsm]))
            P.barrier()

        def load_w3(st, col0s, name):
            wst = Ring([(sbt(st, name + "wst%d" % i, [128, 8, 128], F32), Buf(), name + "w%d" % i) for i in range(2)])
            wbf = sbt(st, name + "wbf", [128, 8, 384], BF16)
            Bw = [Buf(), Buf(), Buf()]
            b3 = sbt(st, name + "b3", [128, 4], F32)
            Bb3 = [Buf(), Buf(), Buf()]
            for ci, c0 in enumerate(col0s):
                t, B, lane = wst.next()
                P.dma(t[:], w_in_t[c0 // 128], w=[B], lane=lane)
                pbb = 5 + (ci % 2)
                for kc in range(8):
                    mm(PS[:, pbb, 100 + ci:101 + ci], t[:, kc, :], shift1[:, kc:kc + 1], kc == 0, kc == 7,
                       r=[B, Bmod], w=[bk[pbb]])
                tt("dve", wbf[:, :, ci * 128:(ci + 1) * 128], t[:],
                   G1.unsqueeze(2).to_broadcast([128, 8, 128]), ALU.mult, r=[B, Bsm], w=[Bw[ci]])
                cp("dve", b3[:, ci:ci + 1], PS[:, pbb, 100 + ci:101 + ci], r=[bk[pbb]], w=[Bb3[ci]])
            return wbf, Bw, b3, Bb3

        class QKN:
            def __init__(self, st, name):
                self.ysb = Ring([(sbt(st, name + "y%d" % i, [128, 512], F32), Buf()) for i in range(2)])
                self.sqb = Ring([(sbt(st, name + "s%d" % i, [128, 512], BF16), Buf()) for i in range(2)])
                self.ln = Ring([(sbt(st, name + "l%d" % i, [128, 512], F32), Buf()) for i in range(2)])
                self.rr = Ring([(sbt(st, name + "r%d" % i, [128, 512], F32), Buf()) for i in range(2)])
                self.nb = 0

            def run(self, pb, N, biascol, gcol, out_ap, rB, wB):
                y, By = self.ysb.next()
                act(y[:, :N], PS[:, pb, :N], AF.Identity, r=[bk[pb]] + rB, w=[By], bias=biascol)
                s, Bs = self.sqb.next()
                tt("pool", s[:, :N], y[:, :N], y[:, :N], ALU.mult, r=[By], w=[Bs])
                self.flush()

                def part2():
                    b2 = 3 + (self.nb % 2)
                    self.nb += 1
                    mm(PS[:, b2, :N], blk_b, s[:, :N], True, True, r=[Bcm, Bs], w=[bk[b2]])
                    l, Bl = self.ln.next()
                    r_, Br = self.rr.next()
                    rsqrt_chain(PS[:, b2, :N], None, 1.0 / 64, l[:, :N], r_[:, :N], [bk[b2]], Bl, Br)
                    stt(out_ap, y[:, :N], gcol, r_[:, :N], ALU.mult, ALU.mult, r=[By, Br, Bsm, Bcols], w=wB)
                self.pending = part2

            def flush(self):
                if getattr(self, "pending", None) is not None:
                    p2 = self.pending
                    self.pending = None
                    p2()

        def proj(pb, wbf, Bw, ci, g, N):
            for kc in range(8):
                mm(PS[:, pb, :N], wbf[:, kc, ci * 128:(ci + 1) * 128], xn_ap(g, N, kc),
                   kc == 0, kc == 7, r=[Bw[ci], Bxn[g]], w=[bk[pb]])

        def phase_A(hp):
            with contextlib.ExitStack() as st:
                QT = sbt(st, "aQT", [128, 4100], BF16)
                KT = sbt(st, "aKT", [128, 5632], BF16)
                VT = sbt(st, "aVT", [128, 5632], BF16)
                BQ, BK, BV = Buf(), Buf(), Buf()
                stp = st.enter_context(contextlib.ExitStack())
                wbf, Bw, b3, Bb3 = load_w3(stp, [128 * hp, 512 + 128 * hp, 1024 + 128 * hp], "a")
                qkn = QKN(stp, "a")
                pbr = 0
                for g in range(11):
                    jobs = [(1, 512), (2, 512)]
                    if g < 8:
                        jobs.insert(0, (0, 512))
                    elif g == 8:
                        jobs.append((0, 1))
                    for ci, N in jobs:
                        pb = pbr % 3
                        pbr += 1
                        proj(pb, wbf, Bw, ci, g, N)
                        if ci == 2:
                            act(VT[:, g * 512:g * 512 + N], PS[:, pb, :N], AF.Identity, r=[bk[pb], Bb3[2]], w=[BV],
                                bias=b3[:, 2:3])
                        elif ci == 1:
                            qkn.run(pb, N, b3[:, 1:2], gk_a, KT[:, g * 512:g * 512 + N], [Bb3[1]], [BK])
                        else:
                            qkn.run(pb, N, b3[:, 0:1], gq_a, QT[:, g * 512:g * 512 + N], [Bb3[0]], [BQ])
                qkn.flush()
                P.barrier()
                stp.close()
                ASTOP = os.environ.get("A_STOP", "")
                if ASTOP == "proj":
                    finals.append(P.dma(otscr_v[:, 0, 0:4100], QT[:, 0:4100], r=[BQ]))
                    finals.append(P.dma(otscr_v[:, 1, 0:4100], KT[:, 0:4100], r=[BK]))
                    finals.append(P.dma(otscr_v[:, 2, 0:4100], VT[:, 0:4100], r=[BV]))
                    P.barrier()
                    return
                accN = sbt(st, "accN", [128, 4100], F32)
                accD = sbt(st, "accD", [128, 4100], F32)
                BaN, BaD = Buf(), Buf()
                bAf = Ring([(sbt(st, "bAf%d" % i, [128, 512], F32), Buf(), "bAf%d" % i) for i in range(2)])
                bA = sbt(st, "bA", [128, 3, 512], BF16)
                BbA = Buf()
                Vt = sbt(st, "aVt", [128, 52, 128], BF16)
                BVt = Buf()
                Et = Ring([(sbt(st, "aEt%d" % i, [128, 512], BF16), Buf()) for i in range(4)])
                for p, (dil, nqb) in enumerate(PATS):
                    t, B, lane = bAf.next()
                    P.dma(t[:], biasA_d[p * 4 + hp], w=[B], lane=lane)
                    act(bA[:, p, :], t[:], AF.Exp, r=[B], w=[BbA])
                for p, (dil, nqb) in enumerate(PATS):
                    if os.environ.get("A_PATS") and str(p) not in os.environ["A_PATS"]:
                        continue
                    tiles = {}
                    lst = []
                    for r in range(dil):
                        for c in range(nqb + 1 + (1 if r == 0 else 0)):
                            tiles[(r, c)] = len(lst)
                            lst.append((r, c))
                    assert len(lst) <= 52

                    def prange(r, c):
                        if c == 0:
                            return 64, 128
                        if c == nqb + 1:
                            return 0, 32
                        return 0, 128
                    for i0 in range(0, len(lst), 8):
                        grp = lst[i0:i0 + 8]
                        tb = 6 + ((i0 // 8) % 2)
                        ptb = PS[:, tb, :].bitcast(BF16)
                        for i, (r, c) in enumerate(grp):
                            plo, phi = prange(r, c)
                            u0 = (128 * c - 64 + plo) * dil + r
                            cnt = phi - plo
                            P.op("pe", (lambda o_, i_: lambda e: e.transpose(out=o_, in_=i_, identity=ident_b))(
                                ptb[plo:phi, i * 128:(i + 1) * 128], VT[:, u0:u0 + (cnt - 1) * dil + 1:dil]),
                                r=[BV, Bcm], w=[bk[tb]])
                        cp("dve", Vt[:, i0:i0 + len(grp), :], ptb[:, 0:len(grp) * 128].rearrange("p (a b) -> p a b", b=128),
                           r=[bk[tb]], w=[BVt])
                    if os.environ.get("A_VTONLY"):
                        continue
                    units = []
                    for r in range(dil):
                        for qb in list(range(nqb)) + ([nqb] if (r == 0 and not os.environ.get("A_NOEXTRA")) else []):
                            units.append((r, qb))

                    def a_s_stage(un, r, qb):
                        N = 128 if qb < nqb else 1
                        sb_ = 2 * (un % 3)
                        uq0 = 128 * qb * dil + r
                        qsl = slice(uq0, uq0 + (N - 1) * dil + 1, dil)
                        chs = []
                        for ch in range(2):
                            c = qb + ch
                            plo, phi = prange(r, c)
                            chs.append((c, plo, phi))
                        for ch, (c, plo, phi) in enumerate(chs):
                            for h in range(2):
                                uk0 = (128 * c - 64 + plo) * dil + r
                                cnt = phi - plo
                                ksl = slice(uk0, uk0 + (cnt - 1) * dil + 1, dil)
                                mm(PS[plo:phi, sb_ + h, ch * 128:ch * 128 + N], KT[64 * h:64 * h + 64, ksl],
                                   QT[64 * h:64 * h + 64, qsl],
                                   True, True, r=[BK, BQ], w=[bk[sb_ + h]], skip_group_check=True)
                        et, Be = Et.next()
                        act(et[:].rearrange("p (a b) -> p a b", a=2), PS[:, sb_:sb_ + 2, 0:256], AF.Exp,
                            r=[bk[sb_], bk[sb_ + 1]], w=[Be])
                        tt("dve", et[:], et[:], bA[:, p, :], ALU.mult, r=[Be, BbA], w=[Be])
                        return (un, r, N, qsl, chs, et, Be)

                    def a_pv_stage(un, r, N, qsl, chs, et, Be):
                        ob_ = 6 + (un % 2)
                        for which in range(2):
                            for ch, (c, plo, phi) in enumerate(chs):
                                for h in range(2):
                                    qd = (h * 2 + ch) * 128
                                    if which == 0:
                                        lh = Vt[plo:phi, tiles[(r, c)], 64 * h:64 * h + 64]
                                    else:
                                        lh = ones_b[plo:phi, 0:64]
                                    mm(PS[64 * h:64 * h + 64, ob_, which * 128:which * 128 + N], lh,
                                       et[plo:phi, qd:qd + N], ch == 0, ch == 1,
                                       r=[BVt, Be, Bcm], w=[bk[ob_]])
                        if p == 0:
                            cp("dve", accN[:, qsl], PS[:, ob_, 0:N], r=[bk[ob_]], w=[BaN])
                            cp("dve", accD[:, qsl], PS[:, ob_, 128:128 + N], r=[bk[ob_]], w=[BaD])
                        else:
                            tt("dve", accN[:, qsl], PS[:, ob_, 0:N], accN[:, qsl], ALU.add, r=[bk[ob_], BaN], w=[BaN])
                            tt("dve", accD[:, qsl], PS[:, ob_, 128:128 + N], accD[:, qsl], ALU.add, r=[bk[ob_], BaD], w=[BaD])

                    inflight = []
                    for un, (r, qb) in enumerate(units):
                        inflight.append(a_s_stage(un, r, qb))
                        if len(inflight) > 2:
                            a_pv_stage(*inflight.pop(0))
                    while inflight:
                        a_pv_stage(*inflight.pop(0))
                ob = sbt(st, "aob", [128, 4100], BF16)
                Bob = Buf()
                for q0 in range(0, NQ, 512):
                    N = min(512, NQ - q0)
                    act(accD[:, q0:q0 + N], accD[:, q0:q0 + N], AF.Ln, r=[BaD], w=[BaD])
                    act(accD[:, q0:q0 + N], accD[:, q0:q0 + N], AF.Exp, r=[BaD], w=[BaD], scale=-1.0)
                    tt("dve", ob[:, q0:q0 + N], accN[:, q0:q0 + N], accD[:, q0:q0 + N], ALU.mult, r=[BaN, BaD], w=[Bob])
                d = P.dma(otscr_v[:, hp, 0:NQ], ob[:, 0:NQ], r=[Bob])
                if debug and stop_after == "A":
                    finals.append(d)
                P.barrier()

        def phase_B(h):
            with contextlib.ExitStack() as st:
                QT = sbt(st, "bQT", [128, 4100], BF16)
                KT = sbt(st, "bKT", [128, S], BF16)
                Vtok = sbt(st, "bVt", [128, 64, 128], BF16)
                BQ, BK, BV = Buf(), Buf(), Buf()
                with contextlib.ExitStack() as stp:
                    wbf, Bw, b3, Bb3 = load_w3(stp, [1536 + 128 * h, 2048 + 128 * h, 2560 + 128 * h], "b")
                    qkn = QKN(stp, "b")
                    vts = Ring([(sbt(stp, "bvts%d" % i, [128, 512], BF16), Buf()) for i in range(2)])
                    pbr = 0
                    for g in range(16):
                        jobs = [(1, 512), (2, 512)]
                        if g < 8:
                            jobs.insert(0, (0, 512))
                        elif g == 8:
                            jobs.append((0, 1))
                        for ci, N in jobs:
                            pb = pbr % 3
                            pbr += 1
                            proj(pb, wbf, Bw, ci, g, N)
                            if ci == 2:
                                vt, Bvt = vts.next()
                                act(vt[:], PS[:, pb, :], AF.Identity, r=[bk[pb], Bb3[2]], w=[Bvt], bias=b3[:, 2:3])
                                tb = 5 + (g % 2)
                                ptb = PS[:, tb, :].bitcast(BF16)
                                for i in range(4):
                                    P.op("pe", (lambda o_, i_: lambda e: e.transpose(out=o_, in_=i_, identity=ident_b))(
                                        ptb[:, i * 128:(i + 1) * 128], vt[:, i * 128:(i + 1) * 128]),
                                        r=[Bvt, Bcm], w=[bk[tb]])
                                cp("dve", Vtok[:, 4 * g:4 * g + 4, :], ptb[:, 0:512].rearrange("p (a b) -> p a b", b=128),
                                   r=[bk[tb]], w=[BV])
                            elif ci == 1:
                                qkn.run(pb, N, b3[:, 1:2], gk_b, KT[:, g * 512:g * 512 + N], [Bb3[1]], [BK])
                            else:
                                qkn.run(pb, N, b3[:, 0:1], gq_b, QT[:, g * 512:g * 512 + N], [Bb3[0]], [BQ])
                    qkn.flush()
                    P.barrier()
                strip = sbt(st, "strip", [128, SW], BF16)
                Bst = Buf()
                with contextlib.ExitStack() as sts:
                    stf = Ring([(sbt(sts, "stf%d" % i, [128, 776], F32), Buf(), "stf%d" % i) for i in range(2)])
                    for i in range(4):
                        t, B, lane = stf.next()
                        P.dma(t[:], stripB_d[h, :, i * 776:(i + 1) * 776], w=[B], lane=lane)
                        cp("pool", strip[:, i * 776:(i + 1) * 776], t[:], r=[B], w=[Bst])
                    P.barrier()
                Et = Ring([(sbt(st, "bEt%d" % i, [128, 2, 512], BF16), (Buf(), Buf())) for i in range(4)])
                e4 = sbt(st, "be4", [128, 2, 512], F32)
                dsb = sbt(st, "bdsb", [64, 512], F32)
                Be4, Bd = Buf(), Buf()
                accs = [(sbt(st, "bacc%d" % i, [128, 2, 512], F32), Buf()) for i in range(2)]
                osq = sbt(st, "bosq", [128, 512], BF16)
                Bosq = Buf()
                obs = Ring([(sbt(st, "bob%d" % i, [128, 512], BF16), Buf(), "bob%d" % i) for i in range(2)])
                chunks = [(q0, min(456, NQ - q0)) for q0 in range(0, NQ, 456)]
                assert sum(n for _, n in chunks) == NQ and len(chunks) == 9
                items = []
                for ci_, (q0, N) in enumerate(chunks):
                    for kb in range(64):
                        items.append((ci_, q0, N, kb))

                def s_stage(i, ci_, q0, N, kb):
                    k0 = 128 * kb
                    D = k0 - q0
                    far_pos = D - (N - 1) >= 1024
                    far_neg = D + 127 <= -1024
                    near = not (far_pos or far_neg)
                    sp = 2 * (i % 2)
                    if near:
                        c0 = SC0 - D
                        assert 0 <= c0 and c0 + N <= SW
                        for t in range(2):
                            mm(PS[:, sp + t, :N], ident_b, strip[:, c0:c0 + N], True, True, r=[Bcm, Bst], w=[bk[sp + t]])
                    for t in range(2):
                        mm(PS[:, sp + t, :N], KT[64 * t:64 * t + 64, k0:k0 + 128], QT[64 * t:64 * t + 64, q0:q0 + N],
                           not near, True, r=[BK, BQ], w=[bk[sp + t]], skip_group_check=near)
                    et, Be = Et.next()
                    if near:
                        act(et[:, :, :N], PS[:, sp:sp + 2, :N], AF.Exp, r=[bk[sp], bk[sp + 1]], w=[Be[0], Be[1]])
                    else:
                        cj = 2 * h + (1 if far_pos else 0)
                        act(et[:, :, :N], PS[:, sp:sp + 2, :N], AF.Exp, r=[bk[sp], bk[sp + 1], Bcols], w=[Be[0], Be[1]],
                            bias=cfar[:, cj:cj + 1])
                    return et, Be

                def pv_stage(ci_, q0, N, kb, et, Be):
                    for t in range(2):
                        mm(PS[:, 4 + t, :N], Vtok[:, kb, :], et[:, t, :N], kb == 0, kb == 63, r=[BV, Be[t]], w=[bk[4 + t]])
                    ac, Bac = accs[ci_ % 2]
                    if kb % 4 == 3:
                        for t in range(2):
                            mm(PS[32 * t:32 * t + 32, 6, :N], ones_b[:, 0:32], et[:, t, :N], kb == 3, kb == 63,
                               r=[Bcm, Be[t]], w=[bk[6]])
                    elif kb == 0:
                        cp("dve", ac[:, :, :N], et[:, :, :N], r=[Be[0], Be[1]], w=[Bac])
                    else:
                        tt("dve", ac[:, :, :N], et[:, :, :N], ac[:, :, :N], ALU.add, r=[Be[0], Be[1], Bac], w=[Bac])

                def fin_s0(ci_, q0, N):
                    act(e4[:, :, :N], PS[:, 4:6, :N], AF.Copy, r=[bk[4], bk[5]], w=[Be4])
                    cp("dve", dsb[:, :N], PS[0:64, 6, :N], r=[bk[6]], w=[Bd])

                def fin_d(t):
                    def f(ci_, q0, N):
                        ac, Bac = accs[ci_ % 2]
                        mm(PS[:, 7, :N], ones_f, ac[:, t, :N], True, False, r=[Bcm, Bac], w=[bk[7]])
                        mm(PS[:, 7, :N], sel_f[t], dsb[:, :N], False, True, r=[Bcm, Bd], w=[bk[7]])
                        P.op("dve", (lambda o_, i_: lambda e: e.reciprocal(out=o_, in_=i_))(ac[:, t, :N], PS[:, 7, :N]),
                             r=[bk[7]], w=[Bac])
                    return f

                def fin_s3(ci_, q0, N):
                    ac, Bac = accs[ci_ % 2]
                    tt("pool", e4[:, :, :N], e4[:, :, :N], ac[:, :, :N], ALU.mult, r=[Be4, Bac], w=[Be4])
                    stt(e4[:, 0, :N], e4[:, 1, :N], neglam, e4[:, 0, :N], ALU.mult, ALU.add, r=[Be4, Bsm], w=[Be4])
                    tt("pool", osq[:, :N], e4[:, 0, :N], e4[:, 0, :N], ALU.mult, r=[Be4], w=[Bosq])

                def fin_s4(ci_, q0, N):
                    ac, Bac = accs[ci_ % 2]
                    mm(PS[:, 7, :N], ones_b, osq[:, :N], True, True, r=[Bcm, Bosq], w=[bk[7]])
                    rsqrt_chain(PS[:, 7, :N], None, 1.0 / 128, ac[:, 0, :N], ac[:, 1, :N], [bk[7]], Bac, Bac)
                    ob, Bob, lob = obs.next()
                    stt(ob[:, :N], e4[:, 0, :N], subg, ac[:, 1, :N], ALU.mult, ALU.mult, r=[Be4, Bac, Bsm], w=[Bob])
                    d = P.dma(otscr_v[:, 4 + h, q0:q0 + N], ob[:, :N], r=[Bob], lane=lob)
                    if debug and stop_after == "B":
                        finals.append(d)

                FIN = [(0, fin_s0), (4, fin_d(0)), (10, fin_d(1)), (16, fin_s3), (22, fin_s4)]
                pend = []

                def run_pend(i):
                    while pend and pend[0][0] <= i:
                        _, fn_, fc, fq0, fN = pend.pop(0)
                        fn_(fc, fq0, fN)
                infl = []

                def retire():
                    pv = infl.pop(0)
                    pv_stage(*pv)
                    return pv
                for i, (ci_, q0, N, kb) in enumerate(items):
                    run_pend(i)
                    infl.append((ci_, q0, N, kb) + s_stage(i, ci_, q0, N, kb))
                    if len(infl) > 2:
                        pv = retire()
                        if pv[3] == 63:
                            for dl, fn_ in FIN:
                                pend.append((i + dl, fn_, pv[0], pv[1], pv[2]))
                            run_pend(i)
                prev = None
                while infl:
                    prev = retire()
                    if prev[3] == 63 and infl:
                        for dl, fn_ in FIN:
                            fn_(prev[0], prev[1], prev[2])
                run_pend(10 ** 9)
                for dl, fn_ in FIN:
                    fn_(prev[0], prev[1], prev[2])
                P.barrier()

        if debug:
            finals.append(P.dma(dbg_xn, xn_ap(0), r=[Bxn[0]]))
        for hp in range(4):
            if stop_after == "xn":
                break
            phase_A(hp)
            if debug and stop_after == "A":
                break
        if stop_after not in ("xn", "A"):
            xn_parts["hi"] = sbt(main, "xn_hi", [128, 8, 2560], BF16)
            build_xn(range(11, 16))
            for h in range(4):
                phase_B(h)
                if debug and stop_after == "B":
                    break
        main.close()
        P.barrier()

        windows = []
        lo = 0
        while lo < OWN:
            hi = min(lo + WIN, OWN)
            windows.append((lo, hi))
            lo = hi

        def stage_W():
            with contextlib.ExitStack() as st:
                wf = Ring([(sbt(st, "wof%d" % i, [128, 8, 128], F32), Buf(), "wof%d" % i) for i in range(2)])
                wob = sbt(st, "wob", [128, 8, DM], BF16)
                Bwo = [Buf() for _ in range(8)]
                for oc in range(8):
                    t, B, lane = wf.next()
                    P.dma(t[:], w_out_t[oc], w=[B], lane=lane)
                    ce = ("dve", "pool", "act")[oc % 3]
                    if ce == "act":
                        act(wob[:, :, oc * 128:(oc + 1) * 128], t[:], AF.Copy, r=[B], w=[Bwo[oc]])
                    else:
                        cp(ce, wob[:, :, oc * 128:(oc + 1) * 128], t[:], r=[B], w=[Bwo[oc]])
                otw = Ring([(sbt(st, "otw%d" % i, [128, 8, 460], BF16), Buf(), "otw%d" % i) for i in range(2)])
                xw = Ring([(sbt(st, "xw%d" % i, [128, 8, 460], F32), Buf(), "xw%d" % i) for i in range(2)])
                hT = Ring([(sbt(st, "hT%d" % i, [128, 8, 460], F32), Buf(), "hT%d" % i) for i in range(2)])
                sq = Ring([(sbt(st, "wsq%d" % i, [128, 8, 460], BF16), Buf()) for i in range(2)])
                pendW = [None]
                ln = Ring([(sbt(st, "wln%d" % i, [128, 460], F32), Buf()) for i in range(2)])
                rr = Ring([(sbt(st, "wrr%d" % i, [128, 460], F32), Buf()) for i in range(2)])
                hn = Ring([(sbt(st, "whn%d" % i, [128, 8, 460], BF16), Buf(), "whn%d" % i) for i in range(2)])
                pbr = 0

                def loadW(wi):
                    lo, hi = windows[wi]
                    t0 = max(lo - 1, 0)
                    t1 = hi + 1
                    N = t1 - t0
                    ot, Bot, lot = otw.next()
                    P.dma(ot[:, :, :N], otscr_v[:, :, t0:t1], w=[Bot], lane=lot)
                    x_, Bx, lx = xw.next()
                    P.dma(x_[:], xw_d[wi], w=[Bx], lane=lx)
                    return ot, Bot, x_, Bx
                nxtW = loadW(0)
                for wi, (lo, hi) in enumerate(windows):
                    t0 = max(lo - 1, 0)
                    t1 = hi + 1
                    N = t1 - t0
                    ot, Bot, x_, Bx = nxtW
                    if wi + 1 < len(windows):
                        nxtW = loadW(wi + 1)
                    h_, Bh, lh = hT.next()
                    for oc in range(8):
                        pb = pbr % 3
                        pbr += 1
                        for kc in range(8):
                            mm(PS[:, pb, :N], wob[:, kc, oc * 128:(oc + 1) * 128], ot[:, kc, :N], kc == 0, kc == 7,
                               r=[Bwo[oc], Bot], w=[bk[pb]])
                        stt(h_[:, oc, :N], PS[:, pb, :N], gate1[:, oc:oc + 1], x_[:, oc, :N], ALU.mult, ALU.add,
                            r=[bk[pb], Bmod, Bx], w=[Bh])
                    s_, Bs = sq.next()
                    act(s_[:, :, :N], h_[:, :, :N], AF.Square, r=[Bh], w=[Bs])
                    if pendW[0] is not None:
                        pendW[0]()

                    def partB(wi=wi, lo=lo, hi=hi, t0=t0, N=N, s_=s_, Bs=Bs, h_=h_, Bh=Bh, lh=lh):
                        nb = 3 + (wi % 2)
                        for oc in range(8):
                            mm(PS[:, nb, :N], ones_b, s_[:, oc, :N], oc == 0, oc == 7, r=[Bcm, Bs], w=[bk[nb]])
                        l_, Bl = ln.next()
                        r_, Br = rr.next()
                        rsqrt_chain(PS[:, nb, :N], None, 1.0 / DM, l_[:, :N], r_[:, :N], [bk[nb]], Bl, Br)
                        n_, Bn, lnn = hn.next()
                        tt("dve", n_[:, :, :N], h_[:, :, :N], r_[:, :N].unsqueeze(1).to_broadcast([128, 8, N]), ALU.mult,
                           r=[Bh, Br], w=[Bn])
                        P.dma(hscr_w[wi], h_[:], r=[Bh], lane=lh + "s")
                        P.dma(nscr_w[wi], n_[:], r=[Bn], lane=lnn + "s")
                    pendW[0] = partB
                pendW[0]()
                P.barrier()

        def stage_F():
            with contextlib.ExitStack() as st:
                wub = sbt(st, "wub", [128, 8, 2 * NFF * 128], BF16)
                wdb = sbt(st, "wdb", [128, NFF, DM], BF16)
                bup = sbt(st, "bup", [128, 2 * NFF], F32)
                Bwu, Bwd, Bbu = Buf(), Buf(), Buf()
                with contextlib.ExitStack() as stw:
                    wf = Ring([(sbt(stw, "wuf%d" % i, [128, 8, 512], F32), Buf(), "wuf%d" % i) for i in range(2)])
                    wdf = Ring([(sbt(stw, "wdf%d" % i, [128, DM], F32), Buf(), "wdf%d" % i) for i in range(2)])
                    for pc in range(NFF // 2):
                        t, B, lane = wf.next()
                        P.dma(t[:], w_up_t[pc], w=[B], lane=lane)
                        for cl in range(4):
                            cc = pc * 4 + cl
                            for kc in range(8):
                                mm(PS[:, 7, cc:cc + 1], t[:, kc, cl * 128:(cl + 1) * 128], shift2[:, kc:kc + 1],
                                   kc == 0, kc == 7, r=[B, Bmod], w=[bk[7]])
                        tt("dve", wub[:, :, pc * 512:(pc + 1) * 512], t[:],
                           G2.unsqueeze(2).to_broadcast([128, 8, 512]), ALU.mult, r=[B, Bsm], w=[Bwu])
                        for pd in (2 * pc, 2 * pc + 1):
                            t, B, lane = wdf.next()
                            P.dma(t[:], w_down_v[:, pd, :], w=[B], lane=lane)
                            cp("pool", wdb[:, pd, :], t[:], r=[B], w=[Bwd])
                    cp("dve", bup[:], PS[:, 7, 0:2 * NFF], r=[bk[7]], w=[Bbu])
                    P.barrier()
                K1 = sbt(st, "fK1", [128, 2 * NFF], F32)
                K0 = sbt(st, "fK0", [128, 2 * NFF], F32)
                BK1 = Buf()
                w3 = cols[:, 81:213].rearrange("p (c t) -> p c t", t=3)
                tt("dve", K1[:], w3[:, :, 0], w3[:, :, 1], ALU.add, r=[Bcols], w=[BK1])
                tt("dve", K1[:], K1[:], w3[:, :, 2], ALU.add, r=[Bcols, BK1], w=[BK1])
                tt("dve", K1[:], K1[:], bup[:], ALU.mult, r=[Bbu, BK1], w=[BK1])
                tt("dve", K1[:], K1[:], cols[:, 213:257], ALU.add, r=[Bcols, BK1], w=[BK1])
                tt("dve", K0[:], bup[:], w3[:, :, 0], ALU.mult, r=[Bbu, Bcols], w=[BK1])
                hw = Ring([(sbt(st, "fhw%d" % i, [128, 8, 460], BF16), Buf(), "fhw%d" % i) for i in range(2)])
                aT = sbt(st, "faT", [128, NFF, 460], BF16)
                BaT = Buf()
                cv = Ring([(sbt(st, "fcv%d" % i, [128, 460], F32), Buf()) for i in range(2)])
                cg = Ring([(sbt(st, "fcg%d" % i, [128, 460], F32), Buf()) for i in range(2)])
                sg = Ring([(sbt(st, "fsg%d" % i, [128, 460], F32), Buf()) for i in range(2)])
                hr = Ring([(sbt(st, "fhr%d" % i, [128, 460], F32), Buf(), "fhr%d" % i) for i in range(8)])
                ot = Ring([(sbt(st, "fot%d" % i, [128, 460], F32), Buf(), "fot%d" % i) for i in range(3)])
                pbr = 0

                def wgeom(wi):
                    lo, hi = windows[wi]
                    t0 = max(lo - 1, 0)
                    t1 = hi + 1
                    return lo, hi, t0, t1, t1 - t0, hi - lo

                def load_hn(wi):
                    lo, hi, t0, t1, N, M = wgeom(wi)
                    hn, Bhn, lhn = hw.next()
                    P.dma(hn[:], nscr_w[wi], w=[Bhn], lane=lhn)
                    return hn, Bhn
                nxt = load_hn(0)
                for wi in range(len(windows)):
                    lo, hi, t0, t1, N, M = wgeom(wi)
                    first = (wi == 0)
                    hn, Bhn = nxt
                    if wi + 1 < len(windows):
                        nxt = load_hn(wi + 1)
                    hres = []
                    for oc in range(8):
                        h_, Bh, lh = hr.next()
                        P.dma(h_[:, :M], hscr_w[wi][:, oc, lo - t0:hi - t0], w=[Bh], lane=lh)
                        hres.append((h_, Bh))
                    for f in range(NFF):
                        res = []
                        for which in range(2):
                            pb = pbr % 6
                            pbr += 1
                            cc = which * NFF + f
                            for kc in range(8):
                                mm(PS[:, pb, :N], wub[:, kc, cc * 128:(cc + 1) * 128], hn[:, kc, :N], kc == 0, kc == 7,
                                   r=[Bwu, Bhn], w=[bk[pb]])
                            c_, Bc = (cv if which == 0 else cg).next()
                            wcol = 81 + cc * 3
                            ctr = 0 if first else 1
                            act(c_[:, :M], PS[:, pb, ctr:ctr + M], AF.Identity, r=[bk[pb], BK1, Bcols], w=[Bc],
                                scale=cols[:, wcol + 1:wcol + 2], bias=K1[:, cc:cc + 1])
                            if first:
                                stt(c_[:, 1:M], PS[:, pb, 0:M - 1], cols[:, wcol:wcol + 1], c_[:, 1:M], ALU.mult, ALU.add,
                                    r=[bk[pb], Bcols, Bc], w=[Bc])
                                ts("dve", c_[:, 0:1], c_[:, 0:1], K0[:, cc:cc + 1], None, ALU.subtract, None, r=[Bc, BK1], w=[Bc])
                            else:
                                stt(c_[:, :M], PS[:, pb, 0:M], cols[:, wcol:wcol + 1], c_[:, :M], ALU.mult, ALU.add,
                                    r=[bk[pb], Bcols, Bc], w=[Bc])
                            stt(c_[:, :M], PS[:, pb, ctr + 1:ctr + 1 + M], cols[:, wcol + 2:wcol + 3], c_[:, :M], ALU.mult, ALU.add,
                                r=[bk[pb], Bcols, Bc], w=[Bc])
                            res.append((c_, Bc))
                        (yv, Byv), (yg, Byg) = res
                        s_, Bs = sg.next()
                        act(s_[:, :M], yg[:, :M], AF.Silu, r=[Byg], w=[Bs])
                        tt("pool", aT[:, f, :M], s_[:, :M], yv[:, :M], ALU.mult, r=[Bs, Byv], w=[BaT])
                    for oc in range(8):
                        pb = 6 + (oc % 2)
                        for f in range(NFF):
                            mm(PS[:, pb, :M], wdb[:, f, oc * 128:(oc + 1) * 128], aT[:, f, :M], f == 0, f == NFF - 1,
                               r=[Bwd, BaT], w=[bk[pb]])
                        h_, Bh = hres[oc]
                        o_, Bo, lo_ = ot.next()
                        stt(o_[:, :M], PS[:, pb, :M], gate2[:, oc:oc + 1], h_[:, :M], ALU.mult, ALU.add,
                            r=[bk[pb], Bmod, Bh], w=[Bo])
                        finals.append(P.dma(outT_v[:, oc, lo:hi], o_[:, :M], r=[Bo], lane=lo_))
                P.barrier()

        if stop_after is None or stop_after in ("W", "F"):
            stage_W()
        if stop_after is None or stop_after == "F":
            stage_F()
        last = {}
        for o in finals:
            last[o.lane] = o if (o.lane not in last or o.lane_cnt > last[o.lane].lane_cnt) else last[o.lane]
        with nc.allow_non_contiguous_dma(reason="single-token halo columns"):
            P.emit(final_wait_ops=list(last.values()))
    return nc


def _t5_bucket(rel):
    nb = 16
    me = 8
    base = np.where(rel > 0, nb, 0)
    n = np.abs(rel)
    nf = np.maximum(n, 1).astype(np.float32)
    large = me + (np.log(nf / np.float32(me)) / np.float32(math.log(2048 / me)) * np.float32(nb - me)).astype(np.int32)
    large = np.minimum(large, nb - 1)
    return base + np.where(n < me, n, large)


def _col(v, n):
    return np.ascontiguousarray(np.asarray(v, np.float32).reshape(n, 128).T)


def _core_inputs(inp, b, flip):
    sg = -1 if flip else 1
    x = inp["x"][b]
    if flip:
        x = x[::-1]
    xT = np.ascontiguousarray(x.T).reshape(8, 128, S)
    cols = np.zeros((128, NCOL), np.float32)
    cols[:, 0:8] = _col(inp["c"][b], 8)
    cols[:, 8:56] = _col(inp["b_ada"][0], 48)
    cols[:, 56:64] = _col(inp["norm1_g"][0], 8)
    cols[:, 64:72] = _col(inp["norm2_g"][0], 8)
    cols[:, 72] = np.tile(inp["q_norm_a"][0], 2)
    cols[:, 73] = np.tile(inp["k_norm_a"][0], 2)
    cols[:, 74] = np.tile(inp["q_norm_b"][0], 2)
    cols[:, 75] = np.tile(inp["k_norm_b"][0], 2)
    cols[:, 76] = inp["subln_g"][0]
    cols[0:64, 77] = inp["lambda_q1"][0]
    cols[0:64, 78] = inp["lambda_k1"][0]
    cols[0:64, 79] = inp["lambda_q2"][0]
    cols[0:64, 80] = inp["lambda_k2"][0]
    cw = inp["conv_w"][0]
    if flip:
        cw = cw[::-1]
    cols[:, 81:213] = np.stack([_col(cw[t], 44) for t in range(3)], axis=2).reshape(128, 132)
    cols[:, 213:257] = _col(inp["conv_b"][0], 44)
    cmat = np.zeros((128, 6, 128), np.float32)
    cmat[0:32, 3, :] = 1.0 / 32
    cmat[32:64, 4, :] = 1.0 / 32
    cmat[:, 0, :] = np.eye(128)
    cmat[:, 1, :] = 1.0
    blk = np.zeros((128, 128), np.float32)
    blk[:64, :64] = 1.0
    blk[64:, 64:] = 1.0
    cmat[:, 2, :] = blk
    table = np.asarray(inp["rel_bias"], np.float32)
    biasA = np.empty((12, 128, 512), np.float32)
    i = np.arange(128)[:, None]
    j = np.arange(128)[None, :]
    for p, (dil, _) in enumerate(PATS):
        for hp in range(4):
            for hl in range(2):
                for ch in range(2):
                    rel = (i - 64 - j) if ch == 0 else (i + 64 - j)
                    valid = (i >= j) if ch == 0 else (i <= j)
                    vals = table[_t5_bucket(sg * rel * dil), 2 * hp + hl]
                    q = hl * 2 + ch
                    biasA[p * 4 + hp][:, q * 128:(q + 1) * 128] = np.where(valid, vals, np.float32(NEGM))
    kl = np.arange(128)[:, None]
    c = np.arange(SW)[None, :]
    bidx = _t5_bucket(sg * (kl - c + SC0))
    stripB = np.stack([table[bidx, 8 + h] for h in range(4)], 0).astype(np.float32)
    cfar = np.empty((128, 8), np.float32)
    for h in range(4):
        cfar[:, 2 * h] = table[_t5_bucket(np.array(sg * -5000))[()], 8 + h]
        cfar[:, 2 * h + 1] = table[_t5_bucket(np.array(sg * 5000))[()], 8 + h]
    def tiled(w, width):
        n = w.shape[1]
        return np.ascontiguousarray(np.asarray(w, np.float32).reshape(8, 128, n // width, width).transpose(2, 1, 0, 3))
    xw = np.zeros((9, 128, 8, 460), np.float32)
    lo = 0
    wi = 0
    while lo < OWN:
        hi = min(lo + WIN, OWN)
        t0 = max(lo - 1, 0)
        t1 = hi + 1
        xw[wi, :, :, :t1 - t0] = xT[:, :, t0:t1].transpose(1, 0, 2)
        lo = hi
        wi += 1
    return {
        "xT": xT, "cols": cols, "cmat": cmat, "xw": xw,
        "w_ada": tiled(inp["w_ada"][0], 384), "w_in": tiled(inp["w_in"][0], 128),
        "w_out": tiled(inp["w_out"][0], 128), "w_up": tiled(inp["w_up"][0], 512),
        "w_down": np.ascontiguousarray(inp["w_down"][0]),
        "biasA": biasA, "stripB": stripB, "cfar": cfar,
    }


_NC_CACHE = {}


def kernel(**inputs):
    inp = {k: np.asarray(v) for k, v in inputs.items()}
    if "nc" not in _NC_CACHE:
        _NC_CACHE["nc"] = build_nc()
    nc = _NC_CACHE["nc"]
    in_maps = []
    for core in range(8):
        b, jj = core // 2, core % 2
        in_maps.append(_core_inputs(inp, b, jj == 1))
    res = run_bass_kernel_spmd(nc, in_maps, core_ids=list(range(8)))
    out = np.empty((4, S, DM), np.float32)
    for core in range(8):
        b, jj = core // 2, core % 2
        oT = np.asarray(res.results[core]["outT"]).reshape(DM, OWN)
        o = oT.T
        if jj == 1:
            out[b, OWN:] = o[::-1]
        else:
            out[b, :OWN] = o
    return out
```

```python
import math
import os
import contextlib
import numpy as np
import concourse.bass as bass
import concourse.mybir as mybir
from concourse.bass_utils import run_bass_kernel_spmd

F32 = mybir.dt.float32
BF16 = mybir.dt.bfloat16
AF = mybir.ActivationFunctionType
ALU = mybir.AluOpType

S = 8192
OWN = 4096
NQ = 4097
DM = 1024
NFF = 22
EPS = 1e-6
NEGM = -30000.0
SW = 3104
SC0 = 1480
NCOL = 257
PATS = ((1, 32), (4, 8), (16, 2))
WIN = 456


class Buf:
    __slots__ = ("name", "last_w", "readers")

    def __init__(self, name=""):
        self.name = name
        self.last_w = None
        self.readers = []


class Op:
    __slots__ = ("eng", "fn", "deps", "idx", "needs_inc", "lane", "lane_cnt", "pos")

    def __init__(self, eng, fn, lane=None):
        self.pos = 0
        self.eng = eng
        self.fn = fn
        self.deps = []
        self.idx = 0
        self.needs_inc = False
        self.lane = lane
        self.lane_cnt = 0


class Prog:
    ENGS = ("pe", "act", "dve", "pool", "sp")

    def __init__(self, nc):
        self.nc = nc
        self.ops = {e: [] for e in self.ENGS}
        self.lanes = {}
        self.lane_last = {}
        self.pending_dmas = []
        self.bar_deps = {}
        self.rot = 0

    def op(self, eng, fn, r=(), w=(), lane=None):
        o = Op(eng, fn, lane)
        deps = []
        for b in r:
            if b.last_w is not None:
                deps.append(b.last_w)
        for b in w:
            if b.last_w is not None:
                deps.append(b.last_w)
            lastr = {}
            for d in b.readers:
                if d.eng == eng and d.lane is None and lane is None:
                    continue
                if d.lane is None and d.eng in ("pe", "act", "dve"):
                    if d.eng not in lastr or d.pos > lastr[d.eng].pos:
                        lastr[d.eng] = d
                else:
                    deps.append(d)
            deps.extend(lastr.values())
        if eng in self.bar_deps:
            deps.extend(self.bar_deps.pop(eng))
        if lane is not None and lane in self.lane_last:
            deps.append(self.lane_last[lane])
        seen = set()
        for d in deps:
            if id(d) in seen:
                continue
            seen.add(id(d))
            if d.eng == "pe" and eng == "pe" and d.lane is None and lane is None:
                continue
            o.deps.append(d)
            d.needs_inc = True
        for b in r:
            b.readers.append(o)
        for b in w:
            b.last_w = o
            b.readers = []
        if lane is not None:
            self.lanes[lane] = self.lanes.get(lane, 0) + 1
            o.lane_cnt = self.lanes[lane]
            o.needs_inc = True
            self.lane_last[lane] = o
            self.pending_dmas.append(o)
        o.pos = len(self.ops[eng])
        self.ops[eng].append(o)
        return o

    def dma(self, out, in_, r=(), w=(), lane=None, q="sp"):
        if lane is None:
            lane = "g%d" % (self.rot % 8)
            self.rot += 1
        return self.op(q, lambda e: e.dma_start(out=out, in_=in_), r=r, w=w, lane=lane)

    def barrier(self):
        lasts = [self.ops[e][-1] for e in self.ENGS if self.ops[e]]
        lasts = [o for o in lasts if o.lane is None] + self.pending_dmas
        self.pending_dmas = []
        for e in self.ENGS:
            self.bar_deps[e] = list(lasts) + self.bar_deps.get(e, [])

    def emit(self, final_wait_ops=()):
        nc = self.nc
        for o in final_wait_ops:
            o.needs_inc = True
        for e in self.ENGS:
            c = 0
            for o in self.ops[e]:
                if o.lane is None and o.needs_inc:
                    c += 1
                    o.idx = c
        lane_names = sorted(self.lanes)
        with contextlib.ExitStack() as st:
            esem = {e: st.enter_context(nc.semaphore("s_" + e)) for e in self.ENGS}
            lsem = {l: st.enter_context(nc.semaphore("l_" + l)) for l in lane_names}
            block = st.enter_context(nc.Block())

            def token(o):
                if o.lane is not None:
                    return ("L" + o.lane, lsem[o.lane], 16 * o.lane_cnt)
                return ("E" + o.eng, esem[o.eng], o.idx)

            def replay(ename, eh):
                seen = {}
                for o in self.ops[ename]:
                    for d in o.deps:
                        key, sem, val = token(d)
                        if seen.get(key, 0) >= val:
                            continue
                        seen[key] = val
                        eh.wait_ge(sem, val)
                    ins = o.fn(eh)
                    if o.needs_inc:
                        if o.lane is not None:
                            ins.then_inc(lsem[o.lane], 16)
                        else:
                            ins.then_inc(esem[ename], 1)
                if ename == "sp":
                    for o in final_wait_ops:
                        key, sem, val = token(o)
                        if seen.get(key, 0) >= val:
                            continue
                        seen[key] = val
                        eh.wait_ge(sem, val)

            @block.tensor
            def _(e):
                replay("pe", e)

            @block.scalar
            def _(e):
                replay("act", e)

            @block.vector
            def _(e):
                replay("dve", e)

            @block.gpsimd
            def _(e):
                replay("pool", e)

            @block.sync
            def _(e):
                replay("sp", e)


class Ring:
    def __init__(self, items):
        self.items = items
        self.i = 0

    def next(self):
        it = self.items[self.i % len(self.items)]
        self.i += 1
        return it


def build_nc(stop_after=None, debug=False):
    nc = bass.Bass("TRN2", target_bir_lowering=False)

    def din(name, shape, dt=F32):
        return nc.dram_tensor(name, shape, dt, kind="ExternalInput").ap()

    xT = din("xT", [8, 128, S])
    cols_d = din("cols", [128, NCOL])
    cmat_d = din("cmat", [128, 6, 128])
    w_ada_t = din("w_ada", [16, 128, 8, 384])
    w_in_t = din("w_in", [24, 128, 8, 128])
    w_out_t = din("w_out", [8, 128, 8, 128])
    w_up_t = din("w_up", [NFF // 2, 128, 8, 512])
    xw_d = din("xw", [9, 128, 8, 460])
    w_down = din("w_down", [NFF * 128, DM])
    biasA_d = din("biasA", [12, 128, 512])
    stripB_d = din("stripB", [4, 128, SW])
    cfar_d = din("cfar", [128, 8])
    outT = nc.dram_tensor("outT", [8, 128, OWN], F32, kind="ExternalOutput").ap()
    skind = dict(kind="ExternalOutput") if debug else {}
    otscr = nc.dram_tensor("otscr", [8, 128, 4104], BF16, **skind).ap()
    hscr = nc.dram_tensor("hscr", [9, 128, 8 * 460], F32, **skind).ap()
    nscr = nc.dram_tensor("nscr", [9, 128, 8 * 460], BF16, **skind).ap()
    if debug:
        dbg_mod = nc.dram_tensor("dbg_mod", [128, 64], F32, kind="ExternalOutput").ap()
        dbg_xn = nc.dram_tensor("dbg_xn", [128, 8, 512], BF16, kind="ExternalOutput").ap()

    w_down_v = w_down.rearrange("(f p) n -> p f n", p=128)
    xT_v = xT.rearrange("k p t -> p k t")
    otscr_v = otscr.rearrange("k p t -> p k t")
    hscr_w = hscr.rearrange("w p (k t) -> w p k t", k=8)
    nscr_w = nscr.rearrange("w p (k t) -> w p k t", k=8)
    outT_v = outT.rearrange("k p t -> p k t")

    P = Prog(nc)
    finals = []

    with contextlib.ExitStack() as top:
        uid = [0]

        def sbt(st, name, shape, dt):
            uid[0] += 1
            return st.enter_context(nc.sbuf_tensor("t%d_%s" % (uid[0], name), shape, dt))

        PS = top.enter_context(nc.psum_tensor("PS", [128, 8, 512], F32))
        bk = [Buf("bk%d" % i) for i in range(8)]

        cols = sbt(top, "cols", [128, NCOL], F32)
        Bcols = Buf("cols")
        cm_f = sbt(top, "cm_f", [128, 6, 128], F32)
        cm_b = sbt(top, "cm_b", [128, 6, 128], BF16)
        Bcm = Buf("cm")
        ident_b = cm_b[:, 0, :]
        ones_b = cm_b[:, 1, :]
        blk_b = cm_b[:, 2, :]
        ones_f = cm_f[:, 1, :]
        sel_f = [cm_f[0:64, 3, :], cm_f[0:64, 4, :]]
        cfar = sbt(top, "cfar", [128, 8], F32)
        modc = sbt(top, "modc", [128, 48], F32)
        Bmod = Buf("mod")
        sm = sbt(top, "sm", [128, 32], F32)
        Bsm = Buf("sm")
        G1 = sm[:, 0:8]
        G2 = sm[:, 8:16]
        gq_a = sm[:, 16:17]
        gq_b = sm[:, 17:18]
        neglam = sm[:, 18:19]
        subg = sm[:, 19:20]
        prod = sm[:, 20:22]
        e12 = sm[:, 22:24]
        c_act = sm[:, 24:32]
        gk_a = cols[:, 73:74]
        gk_b = cols[:, 75:76]
        shift1 = modc[:, 0:8]
        gate1 = modc[:, 16:24]
        shift2 = modc[:, 24:32]
        gate2 = modc[:, 40:48]

        def mm(out, lhsT, rhs, start, stop, r, w, **kw):
            return P.op("pe", lambda e: e.matmul(out, lhsT=lhsT, rhs=rhs, start=start, stop=stop, **kw), r=r, w=w)

        def act(out, in_, func, r, w, **kw):
            return P.op("act", lambda e: e.activation(out=out, in_=in_, func=func, **kw), r=r, w=w)

        def tt(eng, out, in0, in1, op, r, w):
            return P.op(eng, lambda e: e.tensor_tensor(out=out, in0=in0, in1=in1, op=op), r=r, w=w)

        def ts(eng, out, in0, s1, s2, op0, op1, r, w):
            if op1 is None:
                return P.op(eng, lambda e: e.tensor_scalar(out=out, in0=in0, scalar1=s1, scalar2=None, op0=op0), r=r, w=w)
            return P.op(eng, lambda e: e.tensor_scalar(out=out, in0=in0, scalar1=s1, scalar2=s2, op0=op0, op1=op1), r=r, w=w)

        def stt(out, in0, scalar, in1, op0, op1, r, w):
            return P.op("dve", lambda e: e.scalar_tensor_tensor(out=out, in0=in0, scalar=scalar, in1=in1, op0=op0, op1=op1), r=r, w=w)

        def cp(eng, out, in_, r, w):
            return P.op(eng, lambda e: e.tensor_copy(out=out, in_=in_), r=r, w=w)

        def rsqrt_chain(ps_ap, shape, scale, lnt, rr, r, Blnt, Brr):
            act(lnt, ps_ap, AF.Ln, r=r, w=[Blnt], scale=scale, bias=EPS)
            act(rr, lnt, AF.Exp, r=[Blnt], w=[Brr], scale=-0.5)

        P.dma(cols[:], cols_d, w=[Bcols])
        P.dma(cm_f[:], cmat_d, w=[Bcm])
        P.dma(cfar[:], cfar_d, w=[Bcols])
        cp("pool", cm_b[:], cm_f[:], r=[Bcm], w=[Bcm])
        act(c_act, cols[:, 0:8], AF.Silu, r=[Bcols], w=[Bsm])
        main = top.enter_context(contextlib.ExitStack())
        xn_lo = sbt(main, "xn_lo", [128, 8, 5632], BF16)
        xn_parts = {"lo": xn_lo, "hi": None}
        Bxn = [Buf("xn%d" % g) for g in range(16)]

        def xn_ap(g, N=512, kc=None):
            t, gg = (xn_parts["lo"], g) if g < 11 else (xn_parts["hi"], g - 11)
            if kc is None:
                return t[:, :, gg * 512:gg * 512 + N]
            return t[:, kc, gg * 512:gg * 512 + N]

        def build_xn(groups, extra=()):
            extra = list(extra)
            groups = list(groups)
            with contextlib.ExitStack() as st1:
                xs = Ring([(sbt(st1, "xs%d" % i, [128, 8, 512], F32), Buf(), "xs%d" % i) for i in range(3)])
                sq = Ring([(sbt(st1, "xsq%d" % i, [128, 8, 512], BF16), Buf()) for i in range(2)])
                ln = Ring([(sbt(st1, "xln%d" % i, [128, 512], F32), Buf()) for i in range(2)])
                rr = Ring([(sbt(st1, "xrr%d" % i, [128, 512], F32), Buf()) for i in range(2)])
                for _ in range(min(2, len(extra))):
                    extra.pop(0)()
                for gi, g in enumerate(groups):
                    xt, Bx, lane = xs.next()
                    P.dma(xt[:], xT_v[:, :, g * 512:(g + 1) * 512], w=[Bx], lane=lane)
                    sqt, Bs = sq.next()
                    act(sqt[:], xt[:], AF.Square, r=[Bx], w=[Bs])
                    b = g % 2
                    for kc in range(8):
                        mm(PS[:, b, :], ones_b, sqt[:, kc, :], kc == 0, kc == 7, r=[Bcm, Bs], w=[bk[b]])
                    lt, Bl = ln.next()
                    rt, Br = rr.next()
                    rsqrt_chain(PS[:, b, :], None, 1.0 / DM, lt[:], rt[:], [bk[b]], Bl, Br)
                    tt("dve", xn_ap(g), xt[:],
                       rt[:].unsqueeze(1).to_broadcast([128, 8, 512]), ALU.mult, r=[Bx, Br], w=[Bxn[g]])
                    left = len(groups) - gi - 1
                    k = len(extra) if left == 0 else -(-len(extra) // (left + 1))
                    for _ in range(k):
                        extra.pop(0)()
                P.barrier()

        with contextlib.ExitStack() as st0:
            wa = Ring([(sbt(st0, "wa%d" % i, [128, 8, 384], F32), Buf("wa%d" % i), "wa%d" % i) for i in range(2)])

            def ada_piece(pc):
                def f():
                    t, B, lane = wa.next()
                    P.dma(t[:], w_ada_t[pc], w=[B], lane=lane)
                    for cl in range(3):
                        cc = pc * 3 + cl
                        for kc in range(8):
                            mm(PS[:, 7, cc:cc + 1], t[:, kc, cl * 128:(cl + 1) * 128], c_act[:, kc:kc + 1],
                               kc == 0, kc == 7, r=[B, Bsm], w=[bk[7]])
                return f
            build_xn(range(0, 11), extra=[ada_piece(pc) for pc in range(16)])
            tt("dve", modc[:], PS[:, 7, 0:48], cols[:, 8:56], ALU.add, r=[bk[7], Bcols], w=[Bmod])
            stt(G1, modc[:, 8:16], 1.0, cols[:, 56:64], ALU.add, ALU.mult, r=[Bmod, Bcols], w=[Bsm])
            stt(G2, modc[:, 32:40], 1.0, cols[:, 64:72], ALU.add, ALU.mult, r=[Bmod, Bcols], w=[Bsm])
            ts("dve", gq_a, cols[:, 72:73], 0.125, None, ALU.mult, None, r=[Bcols], w=[Bsm])
            ts("dve", gq_b, cols[:, 74:75], 0.125, None, ALU.mult, None, r=[Bcols], w=[Bsm])
            ts("dve", subg, cols[:, 76:77], 0.8, None, ALU.mult, None, r=[Bcols], w=[Bsm])
            tt("dve", prod, cols[:, 77:81:2], cols[:, 78:82:2], ALU.mult, r=[Bcols], w=[Bsm])
            mm(PS[:, 6, 0:2], ones_f, prod, True, True, r=[Bcm, Bsm], w=[bk[6]])
            act(e12, PS[:, 6, 0:2], AF.Exp, r=[bk[6]], w=[Bsm])
            stt(neglam, e12[:, 1:2], 0.2, e12[:, 0:1], ALU.subtract, ALU.subtract, r=[Bsm], w=[Bsm])
            if debug:
                finals.append(P.dma(dbg_mod[:, 0:48], modc[:], r=[Bmod]))
                finals.append(P.dma(dbg_mod[:, 48:64], sm[:, 8:24], r=[Bsm]))
            P.barrier()

        def load_w3(st, col0s, name):
            wst = Ring([(sbt(st, name + "wst%d" % i, [128, 8, 128], F32), Buf(), name + "w%d" % i) for i in range(2)])
            wbf = sbt(st, name + "wbf", [128, 8, 384], BF16)
            Bw = [Buf(), Buf(), Buf()]
            b3 = sbt(st, name + "b3", [128, 4], F32)
            Bb3 = [Buf(), Buf(), Buf()]
            for ci, c0 in enumerate(col0s):
                t, B, lane = wst.next()
                P.dma(t[:], w_in_t[c0 // 128], w=[B], lane=lane)
                pbb = 5 + (ci % 2)
                for kc in range(8):
                    mm(PS[:, pbb, 100 + ci:101 + ci], t[:, kc, :], shift1[:, kc:kc + 1], kc == 0, kc == 7,
                       r=[B, Bmod], w=[bk[pbb]])
                tt("dve", wbf[:, :, ci * 128:(ci + 1) * 128], t[:],
                   G1.unsqueeze(2).to_broadcast([128, 8, 128]), ALU.mult, r=[B, Bsm], w=[Bw[ci]])
                cp("dve", b3[:, ci:ci + 1], PS[:, pbb, 100 + ci:101 + ci], r=[bk[pbb]], w=[Bb3[ci]])
            return wbf, Bw, b3, Bb3

        class QKN:
            def __init__(self, st, name):
                self.ysb = Ring([(sbt(st, name + "y%d" % i, [128, 512], F32), Buf()) for i in range(2)])
                self.sqb = Ring([(sbt(st, name + "s%d" % i, [128, 512], BF16), Buf()) for i in range(2)])
                self.ln = Ring([(sbt(st, name + "l%d" % i, [128, 512], F32), Buf()) for i in range(2)])
                self.rr = Ring([(sbt(st, name + "r%d" % i, [128, 512], F32), Buf()) for i in range(2)])
                self.nb = 0

            def run(self, pb, N, biascol, gcol, out_ap, rB, wB):
                y, By = self.ysb.next()
                act(y[:, :N], PS[:, pb, :N], AF.Identity, r=[bk[pb]] + rB, w=[By], bias=biascol)
                s, Bs = self.sqb.next()
                tt("pool", s[:, :N], y[:, :N], y[:, :N], ALU.mult, r=[By], w=[Bs])
                self.flush()

                def part2():
                    b2 = 3 + (self.nb % 2)
                    self.nb += 1
                    mm(PS[:, b2, :N], blk_b, s[:, :N], True, True, r=[Bcm, Bs], w=[bk[b2]])
                    l, Bl = self.ln.next()
                    r_, Br = self.rr.next()
                    rsqrt_chain(PS[:, b2, :N], None, 1.0 / 64, l[:, :N], r_[:, :N], [bk[b2]], Bl, Br)
                    stt(out_ap, y[:, :N], gcol, r_[:, :N], ALU.mult, ALU.mult, r=[By, Br, Bsm, Bcols], w=wB)
                self.pending = part2

            def flush(self):
                if getattr(self, "pending", None) is not None:
                    p2 = self.pending
                    self.pending = None
                    p2()

        def proj(pb, wbf, Bw, ci, g, N):
            for kc in range(8):
                mm(PS[:, pb, :N], wbf[:, kc, ci * 128:(ci + 1) * 128], xn_ap(g, N, kc),
                   kc == 0, kc == 7, r=[Bw[ci], Bxn[g]], w=[bk[pb]])

        def phase_A(hp):
            with contextlib.ExitStack() as st:
                QT = sbt(st, "aQT", [128, 4100], BF16)
                KT = sbt(st, "aKT", [128, 5632], BF16)
                VT = sbt(st, "aVT", [128, 5632], BF16)
                BQ, BK, BV = Buf(), Buf(), Buf()
                stp = st.enter_context(contextlib.ExitStack())
                wbf, Bw, b3, Bb3 = load_w3(stp, [128 * hp, 512 + 128 * hp, 1024 + 128 * hp], "a")
                qkn = QKN(stp, "a")
                pbr = 0
                for g in range(11):
                    jobs = [(1, 512), (2, 512)]
                    if g < 8:
                        jobs.insert(0, (0, 512))
                    elif g == 8:
                        jobs.append((0, 1))
                    for ci, N in jobs:
                        pb = pbr % 3
                        pbr += 1
                        proj(pb, wbf, Bw, ci, g, N)
                        if ci == 2:
                            act(VT[:, g * 512:g * 512 + N], PS[:, pb, :N], AF.Identity, r=[bk[pb], Bb3[2]], w=[BV],
                                bias=b3[:, 2:3])
                        elif ci == 1:
                            qkn.run(pb, N, b3[:, 1:2], gk_a, KT[:, g * 512:g * 512 + N], [Bb3[1]], [BK])
                        else:
                            qkn.run(pb, N, b3[:, 0:1], gq_a, QT[:, g * 512:g * 512 + N], [Bb3[0]], [BQ])
                qkn.flush()
                P.barrier()
                stp.close()
                ASTOP = os.environ.get("A_STOP", "")
                if ASTOP == "proj":
                    finals.append(P.dma(otscr_v[:, 0, 0:4100], QT[:, 0:4100], r=[BQ]))
                    finals.append(P.dma(otscr_v[:, 1, 0:4100], KT[:, 0:4100], r=[BK]))
                    finals.append(P.dma(otscr_v[:, 2, 0:4100], VT[:, 0:4100], r=[BV]))
                    P.barrier()
                    return
                acc2 = sbt(st, "acc2", [128, 2, 4100], F32)
                accN = acc2[:, 0, :]
                accD = acc2[:, 1, :]
                BaN = BaD = Buf()
                bAf = Ring([(sbt(st, "bAf%d" % i, [128, 512], F32), Buf(), "bAf%d" % i) for i in range(2)])
                bA = sbt(st, "bA", [128, 3, 512], BF16)
                BbA = Buf()
                Vt = sbt(st, "aVt", [128, 52, 128], BF16)
                BVt = Buf()
                Et = Ring([(sbt(st, "aEt%d" % i, [128, 512], BF16), Buf()) for i in range(4)])
                for p, (dil, nqb) in enumerate(PATS):
                    t, B, lane = bAf.next()
                    P.dma(t[:], biasA_d[p * 4 + hp], w=[B], lane=lane)
                    act(bA[:, p, :], t[:], AF.Exp, r=[B], w=[BbA])
                for p, (dil, nqb) in enumerate(PATS):
                    if os.environ.get("A_PATS") and str(p) not in os.environ["A_PATS"]:
                        continue
                    tiles = {}
                    lst = []
                    for r in range(dil):
                        for c in range(nqb + 1 + (1 if r == 0 else 0)):
                            tiles[(r, c)] = len(lst)
                            lst.append((r, c))
                    assert len(lst) <= 52

                    def prange(r, c):
                        if c == 0:
                            return 64, 128
                        if c == nqb + 1:
                            return 0, 32
                        return 0, 128
                    for i0 in range(0, len(lst), 8):
                        grp = lst[i0:i0 + 8]
                        tb = 6 + ((i0 // 8) % 2)
                        ptb = PS[:, tb, :].bitcast(BF16)
                        for i, (r, c) in enumerate(grp):
                            plo, phi = prange(r, c)
                            u0 = (128 * c - 64 + plo) * dil + r
                            cnt = phi - plo
                            P.op("pe", (lambda o_, i_: lambda e: e.transpose(out=o_, in_=i_, identity=ident_b))(
                                ptb[plo:phi, i * 128:(i + 1) * 128], VT[:, u0:u0 + (cnt - 1) * dil + 1:dil]),
                                r=[BV, Bcm], w=[bk[tb]])
                        cp("dve", Vt[:, i0:i0 + len(grp), :], ptb[:, 0:len(grp) * 128].rearrange("p (a b) -> p a b", b=128),
                           r=[bk[tb]], w=[BVt])
                    if os.environ.get("A_VTONLY"):
                        continue
                    units = []
                    for r in range(dil):
                        for qb in list(range(nqb)) + ([nqb] if (r == 0 and not os.environ.get("A_NOEXTRA")) else []):
                            units.append((r, qb))

                    def a_s_stage(un, r, qb):
                        N = 128 if qb < nqb else 1
                        sb_ = 2 * (un % 3)
                        uq0 = 128 * qb * dil + r
                        qsl = slice(uq0, uq0 + (N - 1) * dil + 1, dil)
                        chs = []
                        for ch in range(2):
                            c = qb + ch
                            plo, phi = prange(r, c)
                            chs.append((c, plo, phi))
                        for ch, (c, plo, phi) in enumerate(chs):
                            for h in range(2):
                                uk0 = (128 * c - 64 + plo) * dil + r
                                cnt = phi - plo
                                ksl = slice(uk0, uk0 + (cnt - 1) * dil + 1, dil)
                                mm(PS[plo:phi, sb_ + h, ch * 128:ch * 128 + N], KT[64 * h:64 * h + 64, ksl],
                                   QT[64 * h:64 * h + 64, qsl],
                                   True, True, r=[BK, BQ], w=[bk[sb_ + h]], skip_group_check=True)
                        et, Be = Et.next()
                        act(et[:].rearrange("p (a b) -> p a b", a=2), PS[:, sb_:sb_ + 2, 0:256], AF.Exp,
                            r=[bk[sb_], bk[sb_ + 1]], w=[Be])
                        tt("dve", et[:], et[:], bA[:, p, :], ALU.mult, r=[Be, BbA], w=[Be])
                        return (un, r, N, qsl, chs, et, Be)

                    def a_pv_stage(un, r, N, qsl, chs, et, Be):
                        ob_ = 6 + (un % 2)
                        for which in range(2):
                            for ch, (c, plo, phi) in enumerate(chs):
                                for h in range(2):
                                    qd = (h * 2 + ch) * 128
                                    if which == 0:
                                        lh = Vt[plo:phi, tiles[(r, c)], 64 * h:64 * h + 64]
                                    else:
                                        lh = ones_b[plo:phi, 0:64]
                                    mm(PS[64 * h:64 * h + 64, ob_, which * 128:which * 128 + N], lh,
                                       et[plo:phi, qd:qd + N], ch == 0, ch == 1,
                                       r=[BVt, Be, Bcm], w=[bk[ob_]])
                        pnd = PS[:, ob_, 0:256].rearrange("p (a b) -> p a b", a=2)[:, :, 0:N]
                        if p == 0:
                            cp("dve", acc2[:, :, qsl], pnd, r=[bk[ob_]], w=[BaN])
                        else:
                            tt("dve", acc2[:, :, qsl], pnd, acc2[:, :, qsl], ALU.add, r=[bk[ob_], BaN], w=[BaN])

                    inflight = []
                    for un, (r, qb) in enumerate(units):
                        inflight.append(a_s_stage(un, r, qb))
                        if len(inflight) > 2:
                            a_pv_stage(*inflight.pop(0))
                    while inflight:
                        a_pv_stage(*inflight.pop(0))
                ob = sbt(st, "aob", [128, 4100], BF16)
                Bob = Buf()
                for q0 in range(0, NQ, 512):
                    N = min(512, NQ - q0)
                    act(accD[:, q0:q0 + N], accD[:, q0:q0 + N], AF.Ln, r=[BaD], w=[BaD])
                    act(accD[:, q0:q0 + N], accD[:, q0:q0 + N], AF.Exp, r=[BaD], w=[BaD], scale=-1.0)
                    tt("dve", ob[:, q0:q0 + N], accN[:, q0:q0 + N], accD[:, q0:q0 + N], ALU.mult, r=[BaN, BaD], w=[Bob])
                d = P.dma(otscr_v[:, hp, 0:NQ], ob[:, 0:NQ], r=[Bob])
                if debug and stop_after == "A":
                    finals.append(d)
                P.barrier()

        def phase_B(h):
            with contextlib.ExitStack() as st:
                QT = sbt(st, "bQT", [128, 4100], BF16)
                KT = sbt(st, "bKT", [128, S], BF16)
                Vtok = sbt(st, "bVt", [128, 64, 128], BF16)
                BQ, BK, BV = Buf(), Buf(), Buf()
                with contextlib.ExitStack() as stp:
                    wbf, Bw, b3, Bb3 = load_w3(stp, [1536 + 128 * h, 2048 + 128 * h, 2560 + 128 * h], "b")
                    qkn = QKN(stp, "b")
                    vts = Ring([(sbt(stp, "bvts%d" % i, [128, 512], BF16), Buf()) for i in range(2)])
                    pbr = 0
                    for g in range(16):
                        jobs = [(1, 512), (2, 512)]
                        if g < 8:
                            jobs.insert(0, (0, 512))
                        elif g == 8:
                            jobs.append((0, 1))
                        for ci, N in jobs:
                            pb = pbr % 3
                            pbr += 1
                            proj(pb, wbf, Bw, ci, g, N)
                            if ci == 2:
                                vt, Bvt = vts.next()
                                act(vt[:], PS[:, pb, :], AF.Identity, r=[bk[pb], Bb3[2]], w=[Bvt], bias=b3[:, 2:3])
                                tb = 5 + (g % 2)
                                ptb = PS[:, tb, :].bitcast(BF16)
                                for i in range(4):
                                    P.op("pe", (lambda o_, i_: lambda e: e.transpose(out=o_, in_=i_, identity=ident_b))(
                                        ptb[:, i * 128:(i + 1) * 128], vt[:, i * 128:(i + 1) * 128]),
                                        r=[Bvt, Bcm], w=[bk[tb]])
                                cp("dve", Vtok[:, 4 * g:4 * g + 4, :], ptb[:, 0:512].rearrange("p (a b) -> p a b", b=128),
                                   r=[bk[tb]], w=[BV])
                            elif ci == 1:
                                qkn.run(pb, N, b3[:, 1:2], gk_b, KT[:, g * 512:g * 512 + N], [Bb3[1]], [BK])
                            else:
                                qkn.run(pb, N, b3[:, 0:1], gq_b, QT[:, g * 512:g * 512 + N], [Bb3[0]], [BQ])
                    qkn.flush()
                    P.barrier()
                strip = sbt(st, "strip", [128, SW], BF16)
                Bst = Buf()
                with contextlib.ExitStack() as sts:
                    stf = Ring([(sbt(sts, "stf%d" % i, [128, 776], F32), Buf(), "stf%d" % i) for i in range(2)])
                    for i in range(4):
                        t, B, lane = stf.next()
                        P.dma(t[:], stripB_d[h, :, i * 776:(i + 1) * 776], w=[B], lane=lane)
                        cp("pool", strip[:, i * 776:(i + 1) * 776], t[:], r=[B], w=[Bst])
                    P.barrier()
                Et = Ring([(sbt(st, "bEt%d" % i, [128, 2, 512], BF16), (Buf(), Buf())) for i in range(4)])
                e4 = sbt(st, "be4", [128, 2, 512], F32)
                dsb = sbt(st, "bdsb", [64, 512], F32)
                Be4, Bd = Buf(), Buf()
                accs = [(sbt(st, "bacc%d" % i, [128, 2, 512], F32), Buf()) for i in range(2)]
                osq = sbt(st, "bosq", [128, 512], BF16)
                Bosq = Buf()
                obs = Ring([(sbt(st, "bob%d" % i, [128, 512], BF16), Buf(), "bob%d" % i) for i in range(2)])
                chunks = [(q0, min(456, NQ - q0)) for q0 in range(0, NQ, 456)]
                assert sum(n for _, n in chunks) == NQ and len(chunks) == 9
                items = []
                for ci_, (q0, N) in enumerate(chunks):
                    for kb in range(64):
                        items.append((ci_, q0, N, kb))

                def s_stage(i, ci_, q0, N, kb):
                    k0 = 128 * kb
                    D = k0 - q0
                    far_pos = D - (N - 1) >= 1024
                    far_neg = D + 127 <= -1024
                    near = not (far_pos or far_neg)
                    sp = 2 * (i % 2)
                    if near:
                        c0 = SC0 - D
                        assert 0 <= c0 and c0 + N <= SW
                        for t in range(2):
                            mm(PS[:, sp + t, :N], ident_b, strip[:, c0:c0 + N], True, True, r=[Bcm, Bst], w=[bk[sp + t]])
                    for t in range(2):
                        mm(PS[:, sp + t, :N], KT[64 * t:64 * t + 64, k0:k0 + 128], QT[64 * t:64 * t + 64, q0:q0 + N],
                           not near, True, r=[BK, BQ], w=[bk[sp + t]], skip_group_check=near)
                    et, Be = Et.next()
                    if near:
                        act(et[:, :, :N], PS[:, sp:sp + 2, :N], AF.Exp, r=[bk[sp], bk[sp + 1]], w=[Be[0], Be[1]])
                    else:
                        cj = 2 * h + (1 if far_pos else 0)
                        act(et[:, :, :N], PS[:, sp:sp + 2, :N], AF.Exp, r=[bk[sp], bk[sp + 1], Bcols], w=[Be[0], Be[1]],
                            bias=cfar[:, cj:cj + 1])
                    return et, Be

                def pv_stage(ci_, q0, N, kb, et, Be):
                    for t in range(2):
                        mm(PS[:, 4 + t, :N], Vtok[:, kb, :], et[:, t, :N], kb == 0, kb == 63, r=[BV, Be[t]], w=[bk[4 + t]])
                    ac, Bac = accs[ci_ % 2]
                    if kb % 4 == 3:
                        for t in range(2):
                            mm(PS[32 * t:32 * t + 32, 6, :N], ones_b[:, 0:32], et[:, t, :N], kb == 3, kb == 63,
                               r=[Bcm, Be[t]], w=[bk[6]])
                    elif kb == 0:
                        cp("dve", ac[:, :, :N], et[:, :, :N], r=[Be[0], Be[1]], w=[Bac])
                    else:
                        tt("dve", ac[:, :, :N], et[:, :, :N], ac[:, :, :N], ALU.add, r=[Be[0], Be[1], Bac], w=[Bac])

                def fin_s0(ci_, q0, N):
                    act(e4[:, :, :N], PS[:, 4:6, :N], AF.Copy, r=[bk[4], bk[5]], w=[Be4])
                    cp("dve", dsb[:, :N], PS[0:64, 6, :N], r=[bk[6]], w=[Bd])

                def fin_d(t):
                    def f(ci_, q0, N):
                        ac, Bac = accs[ci_ % 2]
                        mm(PS[:, 7, :N], ones_f, ac[:, t, :N], True, False, r=[Bcm, Bac], w=[bk[7]])
                        mm(PS[:, 7, :N], sel_f[t], dsb[:, :N], False, True, r=[Bcm, Bd], w=[bk[7]])
                        P.op("dve", (lambda o_, i_: lambda e: e.reciprocal(out=o_, in_=i_))(ac[:, t, :N], PS[:, 7, :N]),
                             r=[bk[7]], w=[Bac])
                    return f

                def fin_s3(ci_, q0, N):
                    ac, Bac = accs[ci_ % 2]
                    tt("pool", e4[:, :, :N], e4[:, :, :N], ac[:, :, :N], ALU.mult, r=[Be4, Bac], w=[Be4])
                    stt(e4[:, 0, :N], e4[:, 1, :N], neglam, e4[:, 0, :N], ALU.mult, ALU.add, r=[Be4, Bsm], w=[Be4])
                    tt("pool", osq[:, :N], e4[:, 0, :N], e4[:, 0, :N], ALU.mult, r=[Be4], w=[Bosq])

                def fin_s4(ci_, q0, N):
                    ac, Bac = accs[ci_ % 2]
                    mm(PS[:, 7, :N], ones_b, osq[:, :N], True, True, r=[Bcm, Bosq], w=[bk[7]])
                    rsqrt_chain(PS[:, 7, :N], None, 1.0 / 128, ac[:, 0, :N], ac[:, 1, :N], [bk[7]], Bac, Bac)
                    ob, Bob, lob = obs.next()
                    stt(ob[:, :N], e4[:, 0, :N], subg, ac[:, 1, :N], ALU.mult, ALU.mult, r=[Be4, Bac, Bsm], w=[Bob])
                    d = P.dma(otscr_v[:, 4 + h, q0:q0 + N], ob[:, :N], r=[Bob], lane=lob)
                    if debug and stop_after == "B":
                        finals.append(d)

                FIN = [(0, fin_s0), (4, fin_d(0)), (10, fin_d(1)), (16, fin_s3), (22, fin_s4)]
                pend = []

                def run_pend(i):
                    while pend and pend[0][0] <= i:
                        _, fn_, fc, fq0, fN = pend.pop(0)
                        fn_(fc, fq0, fN)
                infl = []

                def retire():
                    pv = infl.pop(0)
                    pv_stage(*pv)
                    return pv
                for i, (ci_, q0, N, kb) in enumerate(items):
                    run_pend(i)
                    infl.append((ci_, q0, N, kb) + s_stage(i, ci_, q0, N, kb))
                    if len(infl) > 2:
                        pv = retire()
                        if pv[3] == 63:
                            for dl, fn_ in FIN:
                                pend.append((i + dl, fn_, pv[0], pv[1], pv[2]))
                            run_pend(i)
                prev = None
                while infl:
                    prev = retire()
                    if prev[3] == 63 and infl:
                        for dl, fn_ in FIN:
                            fn_(prev[0], prev[1], prev[2])
                run_pend(10 ** 9)
                for dl, fn_ in FIN:
                    fn_(prev[0], prev[1], prev[2])
                P.barrier()

        if debug:
            finals.append(P.dma(dbg_xn, xn_ap(0), r=[Bxn[0]]))
        for hp in range(4):
            if stop_after == "xn":
                break
            phase_A(hp)
            if debug and stop_after == "A":
                break
        if stop_after not in ("xn", "A"):
            xn_parts["hi"] = sbt(main, "xn_hi", [128, 8, 2560], BF16)
            build_xn(range(11, 16))
            for h in range(4):
                phase_B(h)
                if debug and stop_after == "B":
                    break
        main.close()
        P.barrier()

        windows = []
        lo = 0
        while lo < OWN:
            hi = min(lo + WIN, OWN)
            windows.append((lo, hi))
            lo = hi

        def stage_W():
            with contextlib.ExitStack() as st:
                wf = Ring([(sbt(st, "wof%d" % i, [128, 8, 128], F32), Buf(), "wof%d" % i) for i in range(2)])
                wob = sbt(st, "wob", [128, 8, DM], BF16)
                Bwo = [Buf() for _ in range(8)]
                for oc in range(8):
                    t, B, lane = wf.next()
                    P.dma(t[:], w_out_t[oc], w=[B], lane=lane)
                    ce = ("dve", "pool", "act")[oc % 3]
                    if ce == "act":
                        act(wob[:, :, oc * 128:(oc + 1) * 128], t[:], AF.Copy, r=[B], w=[Bwo[oc]])
                    else:
                        cp(ce, wob[:, :, oc * 128:(oc + 1) * 128], t[:], r=[B], w=[Bwo[oc]])
                otw = Ring([(sbt(st, "otw%d" % i, [128, 8, 460], BF16), Buf(), "otw%d" % i) for i in range(2)])
                xw = Ring([(sbt(st, "xw%d" % i, [128, 8, 460], F32), Buf(), "xw%d" % i) for i in range(2)])
                hT = Ring([(sbt(st, "hT%d" % i, [128, 8, 460], F32), Buf(), "hT%d" % i) for i in range(2)])
                sq = Ring([(sbt(st, "wsq%d" % i, [128, 8, 460], BF16), Buf()) for i in range(2)])
                pendW = [None]
                ln = Ring([(sbt(st, "wln%d" % i, [128, 460], F32), Buf()) for i in range(2)])
                rr = Ring([(sbt(st, "wrr%d" % i, [128, 460], F32), Buf()) for i in range(2)])
                hn = Ring([(sbt(st, "whn%d" % i, [128, 8, 460], BF16), Buf(), "whn%d" % i) for i in range(2)])
                pbr = 0

                def loadW(wi):
                    lo, hi = windows[wi]
                    t0 = max(lo - 1, 0)
                    t1 = hi + 1
                    N = t1 - t0
                    ot, Bot, lot = otw.next()
                    P.dma(ot[:, :, :N], otscr_v[:, :, t0:t1], w=[Bot], lane=lot)
                    x_, Bx, lx = xw.next()
                    P.dma(x_[:], xw_d[wi], w=[Bx], lane=lx)
                    return ot, Bot, x_, Bx
                nxtW = loadW(0)
                for wi, (lo, hi) in enumerate(windows):
                    t0 = max(lo - 1, 0)
                    t1 = hi + 1
                    N = t1 - t0
                    ot, Bot, x_, Bx = nxtW
                    if wi + 1 < len(windows):
                        nxtW = loadW(wi + 1)
                    h_, Bh, lh = hT.next()
                    for oc in range(8):
                        pb = pbr % 3
                        pbr += 1
                        for kc in range(8):
                            mm(PS[:, pb, :N], wob[:, kc, oc * 128:(oc + 1) * 128], ot[:, kc, :N], kc == 0, kc == 7,
                               r=[Bwo[oc], Bot], w=[bk[pb]])
                        stt(h_[:, oc, :N], PS[:, pb, :N], gate1[:, oc:oc + 1], x_[:, oc, :N], ALU.mult, ALU.add,
                            r=[bk[pb], Bmod, Bx], w=[Bh])
                    s_, Bs = sq.next()
                    act(s_[:, :, :N], h_[:, :, :N], AF.Square, r=[Bh], w=[Bs])
                    if pendW[0] is not None:
                        pendW[0]()

                    def partB(wi=wi, lo=lo, hi=hi, t0=t0, N=N, s_=s_, Bs=Bs, h_=h_, Bh=Bh, lh=lh):
                        nb = 3 + (wi % 2)
                        for oc in range(8):
                            mm(PS[:, nb, :N], ones_b, s_[:, oc, :N], oc == 0, oc == 7, r=[Bcm, Bs], w=[bk[nb]])
                        l_, Bl = ln.next()
                        r_, Br = rr.next()
                        rsqrt_chain(PS[:, nb, :N], None, 1.0 / DM, l_[:, :N], r_[:, :N], [bk[nb]], Bl, Br)
                        n_, Bn, lnn = hn.next()
                        tt("dve", n_[:, :, :N], h_[:, :, :N], r_[:, :N].unsqueeze(1).to_broadcast([128, 8, N]), ALU.mult,
                           r=[Bh, Br], w=[Bn])
                        P.dma(hscr_w[wi], h_[:], r=[Bh], lane=lh + "s")
                        P.dma(nscr_w[wi], n_[:], r=[Bn], lane=lnn + "s")
                    pendW[0] = partB
                pendW[0]()
                P.barrier()

        def stage_F():
            with contextlib.ExitStack() as st:
                wub = sbt(st, "wub", [128, 8, 2 * NFF * 128], BF16)
                wdb = sbt(st, "wdb", [128, NFF, DM], BF16)
                bup = sbt(st, "bup", [128, 2 * NFF], F32)
                Bwu, Bwd, Bbu = Buf(), Buf(), Buf()
                with contextlib.ExitStack() as stw:
                    wf = Ring([(sbt(stw, "wuf%d" % i, [128, 8, 512], F32), Buf(), "wuf%d" % i) for i in range(2)])
                    wdf = Ring([(sbt(stw, "wdf%d" % i, [128, DM], F32), Buf(), "wdf%d" % i) for i in range(2)])
                    for pc in range(NFF // 2):
                        t, B, lane = wf.next()
                        P.dma(t[:], w_up_t[pc], w=[B], lane=lane)
                        for cl in range(4):
                            cc = pc * 4 + cl
                            for kc in range(8):
                                mm(PS[:, 7, cc:cc + 1], t[:, kc, cl * 128:(cl + 1) * 128], shift2[:, kc:kc + 1],
                                   kc == 0, kc == 7, r=[B, Bmod], w=[bk[7]])
                        tt("dve", wub[:, :, pc * 512:(pc + 1) * 512], t[:],
                           G2.unsqueeze(2).to_broadcast([128, 8, 512]), ALU.mult, r=[B, Bsm], w=[Bwu])
                        for pd in (2 * pc, 2 * pc + 1):
                            t, B, lane = wdf.next()
                            P.dma(t[:], w_down_v[:, pd, :], w=[B], lane=lane)
                            cp("pool", wdb[:, pd, :], t[:], r=[B], w=[Bwd])
                    cp("dve", bup[:], PS[:, 7, 0:2 * NFF], r=[bk[7]], w=[Bbu])
                    P.barrier()
                K1 = sbt(st, "fK1", [128, 2 * NFF], F32)
                K0 = sbt(st, "fK0", [128, 2 * NFF], F32)
                BK1 = Buf()
                w3 = cols[:, 81:213].rearrange("p (c t) -> p c t", t=3)
                tt("dve", K1[:], w3[:, :, 0], w3[:, :, 1], ALU.add, r=[Bcols], w=[BK1])
                tt("dve", K1[:], K1[:], w3[:, :, 2], ALU.add, r=[Bcols, BK1], w=[BK1])
                tt("dve", K1[:], K1[:], bup[:], ALU.mult, r=[Bbu, BK1], w=[BK1])
                tt("dve", K1[:], K1[:], cols[:, 213:257], ALU.add, r=[Bcols, BK1], w=[BK1])
                tt("dve", K0[:], bup[:], w3[:, :, 0], ALU.mult, r=[Bbu, Bcols], w=[BK1])
                hw = Ring([(sbt(st, "fhw%d" % i, [128, 8, 460], BF16), Buf(), "fhw%d" % i) for i in range(2)])
                aT = sbt(st, "faT", [128, NFF, 460], BF16)
                BaT = Buf()
                cv = Ring([(sbt(st, "fcv%d" % i, [128, 460], F32), Buf()) for i in range(2)])
                cg = Ring([(sbt(st, "fcg%d" % i, [128, 460], F32), Buf()) for i in range(2)])
                sg = Ring([(sbt(st, "fsg%d" % i, [128, 460], F32), Buf()) for i in range(2)])
                hr = Ring([(sbt(st, "fhr%d" % i, [128, 460], F32), Buf(), "fhr%d" % i) for i in range(8)])
                ot = Ring([(sbt(st, "fot%d" % i, [128, 460], F32), Buf(), "fot%d" % i) for i in range(3)])
                pbr = 0

                def wgeom(wi):
                    lo, hi = windows[wi]
                    t0 = max(lo - 1, 0)
                    t1 = hi + 1
                    return lo, hi, t0, t1, t1 - t0, hi - lo

                def load_hn(wi):
                    lo, hi, t0, t1, N, M = wgeom(wi)
                    hn, Bhn, lhn = hw.next()
                    P.dma(hn[:], nscr_w[wi], w=[Bhn], lane=lhn)
                    return hn, Bhn
                nxt = load_hn(0)
                for wi in range(len(windows)):
                    lo, hi, t0, t1, N, M = wgeom(wi)
                    first = (wi == 0)
                    hn, Bhn = nxt
                    if wi + 1 < len(windows):
                        nxt = load_hn(wi + 1)
                    hres = []
                    for oc in range(8):
                        h_, Bh, lh = hr.next()
                        P.dma(h_[:, :M], hscr_w[wi][:, oc, lo - t0:hi - t0], w=[Bh], lane=lh)
                        hres.append((h_, Bh))
                    for f in range(NFF):
                        res = []
                        for which in range(2):
                            pb = pbr % 6
                            pbr += 1
                            cc = which * NFF + f
                            for kc in range(8):
                                mm(PS[:, pb, :N], wub[:, kc, cc * 128:(cc + 1) * 128], hn[:, kc, :N], kc == 0, kc == 7,
                                   r=[Bwu, Bhn], w=[bk[pb]])
                            c_, Bc = (cv if which == 0 else cg).next()
                            wcol = 81 + cc * 3
                            ctr = 0 if first else 1
                            act(c_[:, :M], PS[:, pb, ctr:ctr + M], AF.Identity, r=[bk[pb], BK1, Bcols], w=[Bc],
                                scale=cols[:, wcol + 1:wcol + 2], bias=K1[:, cc:cc + 1])
                            if first:
                                stt(c_[:, 1:M], PS[:, pb, 0:M - 1], cols[:, wcol:wcol + 1], c_[:, 1:M], ALU.mult, ALU.add,
                                    r=[bk[pb], Bcols, Bc], w=[Bc])
                                ts("dve", c_[:, 0:1], c_[:, 0:1], K0[:, cc:cc + 1], None, ALU.subtract, None, r=[Bc, BK1], w=[Bc])
                            else:
                                stt(c_[:, :M], PS[:, pb, 0:M], cols[:, wcol:wcol + 1], c_[:, :M], ALU.mult, ALU.add,
                                    r=[bk[pb], Bcols, Bc], w=[Bc])
                            stt(c_[:, :M], PS[:, pb, ctr + 1:ctr + 1 + M], cols[:, wcol + 2:wcol + 3], c_[:, :M], ALU.mult, ALU.add,
                                r=[bk[pb], Bcols, Bc], w=[Bc])
                            res.append((c_, Bc))
                        (yv, Byv), (yg, Byg) = res
                        s_, Bs = sg.next()
                        act(s_[:, :M], yg[:, :M], AF.Silu, r=[Byg], w=[Bs])
                        tt("pool", aT[:, f, :M], s_[:, :M], yv[:, :M], ALU.mult, r=[Bs, Byv], w=[BaT])
                    for oc in range(8):
                        pb = 6 + (oc % 2)
                        for f in range(NFF):
                            mm(PS[:, pb, :M], wdb[:, f, oc * 128:(oc + 1) * 128], aT[:, f, :M], f == 0, f == NFF - 1,
                               r=[Bwd, BaT], w=[bk[pb]])
                        h_, Bh = hres[oc]
                        o_, Bo, lo_ = ot.next()
                        stt(o_[:, :M], PS[:, pb, :M], gate2[:, oc:oc + 1], h_[:, :M], ALU.mult, ALU.add,
                            r=[bk[pb], Bmod, Bh], w=[Bo])
                        finals.append(P.dma(outT_v[:, oc, lo:hi], o_[:, :M], r=[Bo], lane=lo_))
                P.barrier()

        if stop_after is None or stop_after in ("W", "F"):
            stage_W()
        if stop_after is None or stop_after == "F":
            stage_F()
        last = {}
        for o in finals:
            last[o.lane] = o if (o.lane not in last or o.lane_cnt > last[o.lane].lane_cnt) else last[o.lane]
        with nc.allow_non_contiguous_dma(reason="single-token halo columns"):
            P.emit(final_wait_ops=list(last.values()))
    return nc


def _t5_bucket(rel):
    nb = 16
    me = 8
    base = np.where(rel > 0, nb, 0)
    n = np.abs(rel)
    nf = np.maximum(n, 1).astype(np.float32)
    large = me + (np.log(nf / np.float32(me)) / np.float32(math.log(2048 / me)) * np.float32(nb - me)).astype(np.int32)
    large = np.minimum(large, nb - 1)
    return base + np.where(n < me, n, large)


def _col(v, n):
    return np.ascontiguousarray(np.asarray(v, np.float32).reshape(n, 128).T)


def _core_inputs(inp, b, flip):
    sg = -1 if flip else 1
    x = inp["x"][b]
    if flip:
        x = x[::-1]
    xT = np.ascontiguousarray(x.T).reshape(8, 128, S)
    cols = np.zeros((128, NCOL), np.float32)
    cols[:, 0:8] = _col(inp["c"][b], 8)
    cols[:, 8:56] = _col(inp["b_ada"][0], 48)
    cols[:, 56:64] = _col(inp["norm1_g"][0], 8)
    cols[:, 64:72] = _col(inp["norm2_g"][0], 8)
    cols[:, 72] = np.tile(inp["q_norm_a"][0], 2)
    cols[:, 73] = np.tile(inp["k_norm_a"][0], 2)
    cols[:, 74] = np.tile(inp["q_norm_b"][0], 2)
    cols[:, 75] = np.tile(inp["k_norm_b"][0], 2)
    cols[:, 76] = inp["subln_g"][0]
    cols[0:64, 77] = inp["lambda_q1"][0]
    cols[0:64, 78] = inp["lambda_k1"][0]
    cols[0:64, 79] = inp["lambda_q2"][0]
    cols[0:64, 80] = inp["lambda_k2"][0]
    cw = inp["conv_w"][0]
    if flip:
        cw = cw[::-1]
    cols[:, 81:213] = np.stack([_col(cw[t], 44) for t in range(3)], axis=2).reshape(128, 132)
    cols[:, 213:257] = _col(inp["conv_b"][0], 44)
    cmat = np.zeros((128, 6, 128), np.float32)
    cmat[0:32, 3, :] = 1.0 / 32
    cmat[32:64, 4, :] = 1.0 / 32
    cmat[:, 0, :] = np.eye(128)
    cmat[:, 1, :] = 1.0
    blk = np.zeros((128, 128), np.float32)
    blk[:64, :64] = 1.0
    blk[64:, 64:] = 1.0
    cmat[:, 2, :] = blk
    table = np.asarray(inp["rel_bias"], np.float32)
    biasA = np.empty((12, 128, 512), np.float32)
    i = np.arange(128)[:, None]
    j = np.arange(128)[None, :]
    for p, (dil, _) in enumerate(PATS):
        for hp in range(4):
            for hl in range(2):
                for ch in range(2):
                    rel = (i - 64 - j) if ch == 0 else (i + 64 - j)
                    valid = (i >= j) if ch == 0 else (i <= j)
                    vals = table[_t5_bucket(sg * rel * dil), 2 * hp + hl]
                    q = hl * 2 + ch
                    biasA[p * 4 + hp][:, q * 128:(q + 1) * 128] = np.where(valid, vals, np.float32(NEGM))
    kl = np.arange(128)[:, None]
    c = np.arange(SW)[None, :]
    bidx = _t5_bucket(sg * (kl - c + SC0))
    stripB = np.stack([table[bidx, 8 + h] for h in range(4)], 0).astype(np.float32)
    cfar = np.empty((128, 8), np.float32)
    for h in range(4):
        cfar[:, 2 * h] = table[_t5_bucket(np.array(sg * -5000))[()], 8 + h]
        cfar[:, 2 * h + 1] = table[_t5_bucket(np.array(sg * 5000))[()], 8 + h]
    def tiled(w, width):
        n = w.shape[1]
        return np.ascontiguousarray(np.asarray(w, np.float32).reshape(8, 128, n // width, width).transpose(2, 1, 0, 3))
    xw = np.zeros((9, 128, 8, 460), np.float32)
    lo = 0
    wi = 0
    while lo < OWN:
        hi = min(lo + WIN, OWN)
        t0 = max(lo - 1, 0)
        t1 = hi + 1
        xw[wi, :, :, :t1 - t0] = xT[:, :, t0:t1].transpose(1, 0, 2)
        lo = hi
        wi += 1
    return {
        "xT": xT, "cols": cols, "cmat": cmat, "xw": xw,
        "w_ada": tiled(inp["w_ada"][0], 384), "w_in": tiled(inp["w_in"][0], 128),
        "w_out": tiled(inp["w_out"][0], 128), "w_up": tiled(inp["w_up"][0], 512),
        "w_down": np.ascontiguousarray(inp["w_down"][0]),
        "biasA": biasA, "stripB": stripB, "cfar": cfar,
    }


_NC_CACHE = {}


def kernel(**inputs):
    inp = {k: np.asarray(v) for k, v in inputs.items()}
    if "nc" not in _NC_CACHE:
        _NC_CACHE["nc"] = build_nc()
    nc = _NC_CACHE["nc"]
    in_maps = []
    for core in range(8):
        b, jj = core // 2, core % 2
        in_maps.append(_core_inputs(inp, b, jj == 1))
    res = run_bass_kernel_spmd(nc, in_maps, core_ids=list(range(8)))
    out = np.empty((4, S, DM), np.float32)
    for core in range(8):
        b, jj = core // 2, core % 2
        oT = np.asarray(res.results[core]["outT"]).reshape(DM, OWN)
        o = oT.T
        if jj == 1:
            out[b, OWN:] = o[::-1]
        else:
            out[b, :OWN] = o
    return out
```
